# Optimizing a Trainium2 kernel written in Bass

```python
import jax, jax.numpy as jnp
from jax import lax
import numpy as np

D_MODEL = 1024
BATCH = 32
SEQ = 256
DEPTH = 2
DEC_BATCH = 2
DEC_SEQ = 1024
PAST_LEN = 512

GRID_W = 64
POS_BASE = 10000.0
EPS = 1e-6
HG_HEADS = 4
HG_DK = 128
HG_DV = 128
HG_WIDTH = HG_HEADS * HG_DV
HG_CHUNK = 16
SG_GROUPS = 4
SG_WIDTH = 512
SG_GROUP_DIM = SG_WIDTH // SG_GROUPS
SG_CHUNK = 128
POOL_WINDOWS = (2, 4, 8, 16)
POOL_WIDTH = 512
POOL_GROUP_DIM = POOL_WIDTH // len(POOL_WINDOWS)
N_BRANCH = 3
IN_COLS = 5 * HG_WIDTH + 2 * SG_WIDTH + POOL_WIDTH + N_BRANCH * D_MODEL
D_FF = 2816
CONV_WIDTH = 3
N_MOD = 6

kernel_name = "hybrid_hgrn2_sgmlp_pool_diffusion_step"


def rms_norm(x, gain):
    xf = x.astype(jnp.float32)
    y = xf * lax.rsqrt(jnp.mean(xf * xf, axis=-1, keepdims=True) + EPS)
    return (y * gain.astype(jnp.float32)).astype(x.dtype)


def grid_pos_embed(n_tokens, dtype):
    rows = n_tokens // GRID_W
    r = jnp.broadcast_to(jnp.arange(rows, dtype=jnp.float32)[:, None], (rows, GRID_W)).reshape(-1)
    col = jnp.broadcast_to(jnp.arange(GRID_W, dtype=jnp.float32)[None, :], (rows, GRID_W)).reshape(-1)
    quarter = D_MODEL // 4
    omega = 1.0 / (POS_BASE ** (jnp.arange(quarter, dtype=jnp.float32) / quarter))
    ar = r[:, None] * omega[None, :]
    ac = col[:, None] * omega[None, :]
    emb = jnp.concatenate([jnp.sin(ar), jnp.cos(ar), jnp.sin(ac), jnp.cos(ac)], axis=-1)
    return emb.astype(dtype)


def log_forget(z, lb):
    z = z.astype(jnp.float32)
    return jnp.logaddexp(0.0, jnp.log(lb) - z) - jax.nn.softplus(-z)


def gla_chunked(q, k, v, logf, s0):
    B, T, H, K = q.shape
    V = v.shape[-1]
    C = HG_CHUNK
    N = T // C
    q = q.reshape(B, N, C, H, K)
    k = k.reshape(B, N, C, H, K)
    v = v.reshape(B, N, C, H, V)
    b = jnp.cumsum(logf.reshape(B, N, C, H, K), axis=2)
    mask = jnp.tril(jnp.ones((C, C), dtype=bool))
    diff = b[:, :, :, None] - b[:, :, None, :]
    decay = jnp.exp(jnp.where(mask[None, None, :, :, None, None], diff, -jnp.inf))
    scores = jnp.einsum('bnthk,bnshk,bntshk->bnhts', q, k, decay)
    o_intra = jnp.einsum('bnhts,bnshv->bnthv', scores, v)
    b_last = b[:, :, -1]
    k_to_end = k * jnp.exp(b_last[:, :, None] - b)
    update = jnp.einsum('bnshk,bnshv->bnhkv', k_to_end, v)
    chunk_decay = jnp.exp(b_last)

    def step(s, inp):
        a_n, u_n = inp
        return a_n[..., None] * s + u_n, s

    s_fin, s_enter = lax.scan(step, s0.astype(jnp.float32),
                              (jnp.moveaxis(chunk_decay, 1, 0), jnp.moveaxis(update, 1, 0)))
    s_enter = jnp.moveaxis(s_enter, 0, 1)
    o_inter = jnp.einsum('bnthk,bnhkv->bnthv', q * jnp.exp(b), s_enter)
    return (o_intra + o_inter).reshape(B, T, H, V), s_fin


def hgrn2_mixer(zq, zf_fwd, zf_bwd, zi, zg, lb, norm_gain, s0):
    B, T, _ = zq.shape
    q = (jax.nn.silu(zq.astype(jnp.float32)) * HG_DK ** -0.5).reshape(B, T, HG_HEADS, HG_DK)
    v = zi.astype(jnp.float32).reshape(B, T, HG_HEADS, HG_DV)
    outs, finals = [], []
    for d, zf in enumerate((zf_fwd, zf_bwd)):
        logf = log_forget(zf, lb[d]).reshape(B, T, HG_HEADS, HG_DK)
        k = -jnp.expm1(logf)
        if d == 0:
            o, s_fin = gla_chunked(q, k, v, logf, s0[:, d])
        else:
            o, s_fin = gla_chunked(jnp.flip(q, axis=1), jnp.flip(k, axis=1), jnp.flip(v, axis=1),
                                   jnp.flip(logf, axis=1), s0[:, d])
            o = jnp.flip(o, axis=1)
        outs.append(o)
        finals.append(s_fin)
    gate = jax.nn.silu(zg.astype(jnp.float32)).reshape(B, T, HG_HEADS, HG_DV)
    o = rms_norm(outs[0] + outs[1], norm_gain) * gate
    return o.reshape(B, T, HG_WIDTH).astype(zq.dtype), jnp.stack(finals, axis=1)


def chunk_spatial_gating(zu, zv, v_gain, w_s, b_s):
    B, T, _ = zv.shape
    N = T // SG_CHUNK
    u = jax.nn.gelu(zu)
    v = rms_norm(jax.nn.gelu(zv), v_gain).reshape(B, N, SG_CHUNK, SG_GROUPS, SG_GROUP_DIM)
    mixed = jnp.einsum('gts,bnsgc->bntgc', w_s, v) + b_s.T[None, None, :, :, None]
    return u * mixed.reshape(B, T, SG_WIDTH).astype(u.dtype)


def multiscale_pool(zp, w_pool, scale):
    B, T, _ = zp.shape
    pf = zp.astype(jnp.float32)
    csum = jnp.concatenate([jnp.zeros((B, 1, POOL_WIDTH), jnp.float32), jnp.cumsum(pf, axis=1)], axis=1)
    t = jnp.arange(T)
    groups = []
    for gi, w in enumerate(POOL_WINDOWS):
        lo = jnp.clip(t - w // 2, 0, T)
        hi = jnp.clip(t + w // 2, 0, T)
        sl = slice(gi * POOL_GROUP_DIM, (gi + 1) * POOL_GROUP_DIM)
        cs = csum[..., sl]
        mean = (cs[:, hi] - cs[:, lo]) / (hi - lo).astype(jnp.float32)[None, :, None]
        groups.append(mean - pf[..., sl])
    pooled = jnp.stack(groups, axis=2)
    out = jnp.einsum('btgc,gcd->btgd', pooled, w_pool.astype(jnp.float32)).reshape(B, T, POOL_WIDTH)
    return (out * scale.astype(jnp.float32)).astype(zp.dtype)


def conv_ffn(x, w_up, conv_w, conv_b, w_down):
    h = x @ w_up
    hp = jnp.pad(h, ((0, 0), (1, 1), (0, 0)))
    h = hp[:, :-2] * conv_w[0] + hp[:, 1:-1] * conv_w[1] + hp[:, 2:] * conv_w[2] + conv_b
    a, b = jnp.split(h, 2, axis=-1)
    return (jax.nn.silu(a) * b) @ w_down


def trunk_layer(x, mod, s0, lb, p):
    shift1, scale1, gate1, shift2, scale2, gate2 = jnp.split(mod[:, None, :].astype(x.dtype), N_MOD, axis=-1)
    h = rms_norm(x, p['norm_mix']) * (1 + scale1) + shift1
    z = h @ p['w_in']
    widths = (HG_WIDTH,) * 5 + (SG_WIDTH,) * 2 + (POOL_WIDTH,)
    offsets = np.cumsum(widths).tolist()
    zq, zf_f, zf_b, zi, zg, zu, zv, zp, zgate = jnp.split(z, offsets, axis=-1)
    o_hg, s_fin = hgrn2_mixer(zq, zf_f, zf_b, zi, zg, lb, p['hg_norm'], s0)
    o_sg = chunk_spatial_gating(zu, zv, p['sg_norm'], p['sg_w'], p['sg_b'])
    o_pool = multiscale_pool(zp, p['pool_w'], p['pool_scale'])
    gates = jax.nn.sigmoid(zgate.astype(jnp.float32)).astype(x.dtype)
    g_hg, g_sg, g_pool = jnp.split(gates, N_BRANCH, axis=-1)
    merged = (g_hg * (o_hg @ p['w_branch_hg']) + g_sg * (o_sg @ p['w_branch_sg'])
              + g_pool * (o_pool @ p['w_branch_pool']))
    x = x + gate1 * (merged @ p['w_out'])
    h = rms_norm(x, p['norm_ffn']) * (1 + scale2) + shift2
    x = x + gate2 * conv_ffn(h, p['ffn_up'], p['ffn_conv_w'], p['ffn_conv_b'], p['ffn_down'])
    return x, s_fin


def setup_inputs(seed: int = 0) -> dict:
    key = jax.random.key(seed)
    ks = jax.random.split(key, 32)
    f32 = jnp.float32
    nrm = lambda k, shape, s: jax.random.normal(k, shape, f32) * s
    D = D_MODEL
    return {
        'x_prompt': nrm(ks[0], (BATCH, SEQ, D), 1.0),
        'x_sample': nrm(ks[1], (DEC_BATCH, DEC_SEQ, D), 1.0),
        'c': nrm(ks[2], (DEC_BATCH, D), 1.0),
        'state_hgrn': nrm(ks[3], (DEC_BATCH, DEPTH, 2, HG_HEADS, HG_DK, HG_DV), 0.5),
        'c_ctx': nrm(ks[4], (D,), 1.0),
        'norm_mix': 1.0 + nrm(ks[5], (DEPTH, D), 0.05),
        'norm_ffn': 1.0 + nrm(ks[6], (DEPTH, D), 0.05),
        'w_ada': nrm(ks[7], (DEPTH, D, N_MOD * D), 0.5 * D ** -0.5),
        'b_ada': nrm(ks[8], (DEPTH, N_MOD * D), 0.02),
        'w_in': nrm(ks[9], (DEPTH, D, IN_COLS), D ** -0.5),
        'lb_logits': nrm(ks[10], (DEPTH, 2, HG_WIDTH), 0.5),
        'hg_norm': 1.0 + nrm(ks[11], (DEPTH, HG_DV), 0.05),
        'w_branch_hg': nrm(ks[12], (DEPTH, HG_WIDTH, D), HG_WIDTH ** -0.5),
        'w_branch_sg': nrm(ks[13], (DEPTH, SG_WIDTH, D), SG_WIDTH ** -0.5),
        'w_branch_pool': nrm(ks[14], (DEPTH, POOL_WIDTH, D), POOL_WIDTH ** -0.5),
        'w_out': nrm(ks[15], (DEPTH, D, D), D ** -0.5),
        'sg_norm': 1.0 + nrm(ks[16], (DEPTH, SG_WIDTH), 0.05),
        'sg_w': nrm(ks[17], (DEPTH, SG_GROUPS, SG_CHUNK, SG_CHUNK), SG_CHUNK ** -0.5),
        'sg_b': 1.0 + nrm(ks[18], (DEPTH, SG_GROUPS, SG_CHUNK), 0.1),
        'pool_w': nrm(ks[19], (DEPTH, len(POOL_WINDOWS), POOL_GROUP_DIM, POOL_GROUP_DIM), POOL_GROUP_DIM ** -0.5),
        'pool_scale': 1.0 + nrm(ks[20], (DEPTH, POOL_WIDTH), 0.1),
        'ffn_up': nrm(ks[21], (DEPTH, D, 2 * D_FF), D ** -0.5),
        'ffn_conv_w': nrm(ks[22], (DEPTH, CONV_WIDTH, 2 * D_FF), CONV_WIDTH ** -0.5),
        'ffn_conv_b': nrm(ks[23], (DEPTH, 2 * D_FF), 0.02),
        'ffn_down': nrm(ks[24], (DEPTH, D_FF, D), D_FF ** -0.5),
        'final_norm': 1.0 + nrm(ks[25], (D,), 0.05),
    }


def reference(x_prompt, x_sample, c, state_hgrn, c_ctx, norm_mix, norm_ffn, w_ada, b_ada, w_in,
              lb_logits, hg_norm, w_branch_hg, w_branch_sg, w_branch_pool, w_out, sg_norm, sg_w, sg_b,
              pool_w, pool_scale, ffn_up, ffn_conv_w, ffn_conv_b, ffn_down, final_norm):
    lb_all = jnp.cumsum(jax.nn.softmax(lb_logits.astype(jnp.float32), axis=0), axis=0)
    lower = lb_all - lb_all[0]
    xp = x_prompt
    xs = x_sample + grid_pos_embed(x_sample.shape[1], x_sample.dtype)[None]
    n_ctx = x_prompt.shape[0]
    new_states = []
    for l in range(DEPTH):
        p = {'norm_mix': norm_mix[l], 'norm_ffn': norm_ffn[l], 'w_in': w_in[l], 'hg_norm': hg_norm[l],
             'w_branch_hg': w_branch_hg[l], 'w_branch_sg': w_branch_sg[l], 'w_branch_pool': w_branch_pool[l],
             'w_out': w_out[l], 'sg_norm': sg_norm[l], 'sg_w': sg_w[l], 'sg_b': sg_b[l],
             'pool_w': pool_w[l], 'pool_scale': pool_scale[l], 'ffn_up': ffn_up[l],
             'ffn_conv_w': ffn_conv_w[l], 'ffn_conv_b': ffn_conv_b[l], 'ffn_down': ffn_down[l]}
        mod_ctx = jax.nn.silu(c_ctx)[None, :] @ w_ada[l] + b_ada[l]
        mod_lat = jax.nn.silu(c) @ w_ada[l] + b_ada[l]
        s_zero = jnp.zeros((n_ctx, 2, HG_HEADS, HG_DK, HG_DV), jnp.float32)
        xp, s_ctx = trunk_layer(xp, mod_ctx, s_zero, lower[l], p)
        xs, _ = trunk_layer(xs, mod_lat, state_hgrn[:, l], lower[l], p)
        new_states.append(s_ctx)
    y_prompt = rms_norm(xp, final_norm)
    y_sample = rms_norm(xs, final_norm)
    new_state_hgrn = jnp.stack(new_states, axis=1).astype(x_prompt.dtype)
    return (y_prompt, y_sample, new_state_hgrn)
```

```python
import numpy as np
import concourse.bass as bass
import concourse.mybir as mybir
from concourse.bass_utils import run_bass_kernel_spmd
from contextlib import ExitStack

F32 = mybir.dt.float32
BF16 = mybir.dt.bfloat16
AF = mybir.ActivationFunctionType
ALU = mybir.AluOpType

D = 1024
L = 2
NCORE = 8
NSEG = 5
SEG = 256
NT = NSEG * SEG
NBLK = NT // 128
CH = 32
NCHK = NT // CH
UW = 1296
TILES = [(0, 512), (512, 512), (1024, 256)]
EPS = 1e-6
DFF = 2816
NFC = DFF // 128
SLOT_E = 1024
NSLOT = 8
HG_SCALE = 128 ** -0.5
HG_GROUPS = ((0, 1), (2, 3))
FFN_C3_ENG = "dve"
KTM_ON_POOL = False
KTM_ON_ACT = True
FFN_MUL_ENG = "dve"

DEBUG = {}


class Sem:
    def __init__(self, h, name):
        self.h = h
        self.name = name
        self.val = 0


class T:
    def __init__(self, name):
        self.name = name
        self.w = None
        self.r = {}


class Buf:
    def __init__(self, ap, tiles):
        self.ap = ap
        self.tiles = tiles


def _tiles(xs):
    out = []
    for x in xs:
        if x is None:
            continue
        if isinstance(x, T):
            out.append(x)
        elif isinstance(x, Buf):
            out.extend(x.tiles)
        elif isinstance(x, (list, tuple)):
            out.extend(_tiles(x))
        else:
            raise TypeError(type(x))
    return out


class Prog:
    ENG = ("pe", "act", "dve", "pool", "sp")

    def __init__(self, nc, stack):
        self.nc = nc
        self.q = {e: [] for e in self.ENG}
        self.sem = {}
        for e in ("pe", "act", "dve", "pool"):
            self.sem[e] = Sem(stack.enter_context(nc.semaphore("c_" + e)), e)
        self.known = {e: {} for e in self.ENG}
        self.pending = {e: False for e in self.ENG}
        self.stack = stack
        self.nsem = 4

    def new_sem(self, name):
        self.nsem += 1
        return Sem(self.stack.enter_context(self.nc.semaphore(name)), name)

    def _deps(self, eng, reads, writes):
        deps = {}

        def add(s, v):
            if deps.get(s, 0) < v:
                deps[s] = v

        for t in reads:
            if t.w is not None:
                add(*t.w)
        for t in writes:
            if t.w is not None:
                add(*t.w)
            for s, v in t.r.items():
                add(s, v)
        waits = []
        mysem = self.sem.get(eng)
        for s, v in deps.items():
            if s is mysem and eng == "pe":
                continue
            if self.known[eng].get(s, 0) >= v:
                continue
            self.known[eng][s] = v
            waits.append((s, v))
        return waits

    def op(self, eng, fn, reads=(), writes=(), mark=None):
        reads = _tiles(reads)
        writes = _tiles(writes)
        waits = self._deps(eng, reads, writes)
        s = self.sem[eng]
        if mark is None:
            mark = eng != "pe"
        n = s.val + 1
        if mark:
            s.val = n
            self.pending[eng] = False
        else:
            self.pending[eng] = True
        self.q[eng].append((waits, fn, (s, 1) if mark else None))
        for t in writes:
            t.w = (s, n)
            t.r = {}
        for t in reads:
            if t not in writes:
                t.r[s] = max(t.r.get(s, 0), n)

    def dma(self, eng, out, in_, sem, reads=(), writes=()):
        reads = _tiles(reads)
        writes = _tiles(writes)
        waits = self._deps(eng, reads, writes)
        sem.val += 16
        v = sem.val
        self.q[eng].append((waits, lambda e: e.dma_start(out=out, in_=in_), (sem, 16)))
        for t in writes:
            t.w = (sem, v)
            t.r = {}
        for t in reads:
            t.r[sem] = v

    def wait_all(self, eng, sems):
        for s in sems:
            if s.val > 0 and self.known[eng].get(s, 0) < s.val:
                self.known[eng][s] = s.val
                self.q[eng].append(([(s, s.val)], None, None))

    def emit(self):
        nc = self.nc
        for e in ("pe", "act", "dve", "pool"):
            assert not self.pending[e], "engine %s ends with unmarked op" % e
        with nc.Block() as block:

            def run(eng_name):
                def body(eng):
                    for waits, fn, inc in self.q[eng_name]:
                        for s, v in waits:
                            eng.wait_ge(s.h, v)
                        if fn is None:
                            continue
                        ins = fn(eng)
                        if inc is not None:
                            ins.then_inc(inc[0].h, inc[1])

                return body

            block.tensor(run("pe"))
            block.scalar(run("act"))
            block.vector(run("dve"))
            block.gpsimd(run("pool"))
            block.sync(run("sp"))


class UnitPool:
    def __init__(self, prog, big, nu):
        self.big = big
        self.nu = nu
        self.tiles = [T("u%d" % i) for i in range(nu)]
        self.free = [True] * nu
        self.peak = 0

    def alloc(self, n=1):
        for i in range(0, self.nu - n + 1):
            if all(self.free[i:i + n]):
                for j in range(i, i + n):
                    self.free[j] = False
                self.peak = max(self.peak, self.nu - sum(self.free))
                return (i, n)
        raise RuntimeError("unit pool exhausted (n=%d, free=%d)" % (n, sum(self.free)))

    def release(self, u):
        i, n = u
        for j in range(i, i + n):
            assert not self.free[j]
            self.free[j] = True

    def bf(self, u):
        i, n = u
        return Buf(self.big[:, i * UW:(i + n) * UW], self.tiles[i:i + n])

    def f32(self, u):
        i, n = u
        assert n % 2 == 0
        return Buf(self.big[:, i * UW:(i + n) * UW].bitcast(F32), self.tiles[i:i + n])


def sub(buf, ap):
    return Buf(ap, buf.tiles)


def _fm(v):
    v = np.asarray(v, np.float32)
    return np.ascontiguousarray(v.reshape(-1, 128).T)


def _w_slot_k8(w):
    return np.ascontiguousarray(w.reshape(8, 128, 128).transpose(1, 0, 2).reshape(128, 1024))


def _w_slot_k4(w):
    return np.ascontiguousarray(w.reshape(4, 128, 256).transpose(1, 0, 2).reshape(128, 1024))


class CF:
    def __init__(self):
        self.n = 0
        self.off = {}

    def add(self, name, width):
        self.off[name] = self.n
        self.n += width
        return self.off[name]


def build_layout():
    cf = CF()
    for l in range(L):
        cf.add("norm_mix%d" % l, 8)
        cf.add("norm_ffn%d" % l, 8)
        cf.add("b_ada%d" % l, 48)
        cf.add("hg_norm%d" % l, 1)
        cf.add("sg_norm%d" % l, 4)
        cf.add("pool_scale%d" % l, 4)
        cf.add("conv_w%d" % l, 3 * 44)
        cf.add("conv_b%d" % l, 44)
    cf.add("lb", L * 2 * 4)
    cf.add("final_norm", 8)
    cf.add("flags", 7 * NSEG)
    cf.add("cT", 8 * NSEG)
    cf.add("jidx", 2)
    cf.add("cmf", 4)
    cf.add("bbc", L * 4 * 128)
    cf.add("s0", L * 8 * 128)
    cb = CF()
    cb.add("bmain", 40 * 128)
    cb.add("bprev", 40 * 8)
    cb.add("bnext", 40 * 8)
    cb.add("maskF", 128)
    cb.add("maskB", 128)
    cb.add("cm", 4)
    cb.add("sgw", L * 4 * 128)
    cb.add("poolw", L * 4 * 128)
    cb.add("ident", 128)
    cb.add("scanm", NT)
    return cf, cb


CFL, CBL = build_layout()
POOL_W = (2, 4, 8, 16)


def ada_late(u):
    return range(16 + (u * 32) // 12, 16 + ((u + 1) * 32) // 12)


def weight_order(l):
    o = []
    for j in range(16):
        o.append(("ada", j))
    for j in range(4):
        o.append(("in", 12 + j))
    for grp in HG_GROUPS:
        for h in grp:
            o.append(("in", 0 + h))
        for h in grp:
            o += [("in", 4 + h), ("in", 8 + h)]
        for h in grp:
            o.append(("in", 16 + h))
    for j in range(4):
        o.append(("in", 20 + j))
    for j in range(4):
        o.append(("in", 24 + j))
    for j in range(4):
        o.append(("in", 28 + j))
    u = 0
    for j in range(4):
        for br in range(3):
            o.append(("br%d" % br, j))
            o.append(("in", 32 + br * 8 + 2 * j))
            o.append(("in", 32 + br * 8 + 2 * j + 1))
            for jj in ada_late(u):
                o.append(("ada", jj))
            u += 1
    for j in range(8):
        o.append(("out", j))
    for j in range(NFC):
        o.append(("up", j))
        o.append(("up", NFC + j))
    for d in range(8):
        for g in range(3):
            o.append(("down", d * 3 + g))
    return o


def host_weights(inp):
    slots = []
    for l in range(L):
        for kind, j in weight_order(l):
            if kind == "ada":
                s = _w_slot_k8(inp["w_ada"][l][:, j * 128:(j + 1) * 128])
            elif kind == "in":
                s = _w_slot_k8(inp["w_in"][l][:, j * 128:(j + 1) * 128])
            elif kind.startswith("br"):
                w = (inp["w_branch_hg"], inp["w_branch_sg"], inp["w_branch_pool"])[int(kind[2])][l]
                s = _w_slot_k4(w[:, j * 256:(j + 1) * 256])
            elif kind == "out":
                s = _w_slot_k8(inp["w_out"][l][:, j * 128:(j + 1) * 128])
            elif kind == "up":
                s = _w_slot_k8(inp["ffn_up"][l][:, j * 128:(j + 1) * 128])
            elif kind == "down":
                d, g = divmod(j, 3)
                wd = inp["ffn_down"][l][:, d * 128:(d + 1) * 128]
                blk = np.zeros((1024, 128), np.float32)
                k0 = g * 1024
                k1 = min(DFF, k0 + 1024)
                blk[:k1 - k0] = wd[k0:k1]
                s = _w_slot_k8(blk)
            slots.append(s)
    return np.ascontiguousarray(np.stack(slots, 0))


def band_matrices(seq_start, seq_end, w):
    hw = w // 2
    main = np.zeros((128, 128), np.float32)
    prev = np.zeros((128, 8), np.float32)
    nxt = np.zeros((128, 8), np.float32)
    for t in range(128):
        lo = t - hw
        hi = t + hw
        if seq_start:
            lo = max(lo, 0)
        if seq_end:
            hi = min(hi, 128)
        cnt = hi - lo
        for s in range(lo, hi):
            if s < 0:
                prev[128 + s, t] += 1.0 / cnt
            elif s >= 128:
                nxt[s - 128, t - 120] += 1.0 / cnt
            else:
                main[s, t] += 1.0 / cnt
        main[t, t] -= 1.0
    return main, prev, nxt


def core_segments(core):
    if core < 2:
        return [("s", core, i * SEG) for i in range(4)] + [("p", 30 + core, 0)]
    return [("p", (core - 2) * 5 + i, 0) for i in range(5)]


def host_consts(inp, core):
    segs = core_segments(core)
    cf = np.zeros((128, CFL.n), np.float32)
    o = CFL.off
    for l in range(L):
        cf[:, o["norm_mix%d" % l]:][:, :8] = _fm(inp["norm_mix"][l])
        cf[:, o["norm_ffn%d" % l]:][:, :8] = _fm(inp["norm_ffn"][l])
        cf[:, o["b_ada%d" % l]:][:, :48] = _fm(inp["b_ada"][l])
        cf[:, o["hg_norm%d" % l]:][:, :1] = _fm(inp["hg_norm"][l])
        cf[:, o["sg_norm%d" % l]:][:, :4] = _fm(inp["sg_norm"][l])
        cf[:, o["pool_scale%d" % l]:][:, :4] = _fm(inp["pool_scale"][l])
        for k in range(3):
            cf[:, o["conv_w%d" % l] + k * 44:][:, :44] = _fm(inp["ffn_conv_w"][l][k])
        cf[:, o["conv_b%d" % l]:][:, :44] = _fm(inp["ffn_conv_b"][l])
        for d in range(2):
            cf[:, o["lb"] + (l * 2 + d) * 4:][:, :4] = _fm(inp["lb_logits"][l][d])
        bb = np.asarray(inp["sg_b"][l], np.float32)
        cf[:, o["bbc"] + l * 512:][:, :512] = np.broadcast_to(bb.reshape(1, 512), (128, 512))
    cf[:, o["final_norm"]:][:, :8] = _fm(inp["final_norm"])
    fl = np.zeros((7, NSEG), np.float32)
    for s, (kind, idx, off) in enumerate(segs):
        if kind == "s":
            fl[0, s] = 1.0
            fl[1, s] = 1.0 if off > 0 else 0.0
            fl[2, s] = 1.0 if off == 0 else 0.0
            fl[3, s] = 1.0 if off < 768 else 0.0
            fl[4, s] = 1.0 if off == 768 else 0.0
            fl[5, s] = 1.0 if off > 0 else 0.0
            fl[6, s] = 1.0 if off < 768 else 0.0
    assert fl[0, 0] == fl[0, 1] == fl[0, 2] == fl[0, 3]
    cf[:, o["flags"]:][:, :7 * NSEG] = np.broadcast_to(fl.reshape(1, -1), (128, 7 * NSEG))
    cv = np.stack([inp["c"][idx] if kind == "s" else inp["c_ctx"] for kind, idx, off in segs], 0)
    cf[:, o["cT"]:][:, :40] = np.asarray(cv, np.float32).T.reshape(8, 128, NSEG).transpose(1, 0, 2).reshape(128, 40)
    cf[:, o["cmf"]:][:, :4] = ((np.arange(128)[:, None] // CH) == np.arange(4)[None, :]).astype(np.float32)
    cf[:, o["jidx"]] = np.arange(128)
    cf[:, o["jidx"] + 1] = np.arange(128) + 128
    if core < 2:
        st = np.asarray(inp["state_hgrn"][core], np.float32)
        cf[:, o["s0"]:][:, :L * 8 * 128] = st.reshape(L * 8, 128, 128).transpose(1, 0, 2).reshape(128, -1)

    cb = np.zeros((128, CBL.n), np.float32)
    ob = CBL.off
    for b in range(NBLK):
        s = b // 2
        kind, idx, off = segs[s]
        if kind == "s":
            pos = off + (b % 2) * 128
            st_, en_ = pos == 0, pos == 896
        else:
            st_, en_ = (b % 2 == 0), (b % 2 == 1)
        for g, w in enumerate(POOL_W):
            m, p, n = band_matrices(st_, en_, w)
            cb[:, ob["bmain"] + (b * 4 + g) * 128:][:, :128] = m
            cb[:, ob["bprev"] + (b * 4 + g) * 8:][:, :8] = p
            cb[:, ob["bnext"] + (b * 4 + g) * 8:][:, :8] = n
    ss = np.arange(128)[:, None]
    tt = np.arange(128)[None, :]
    same = (ss // CH) == (tt // CH)
    cb[:, ob["maskF"]:][:, :128] = (same & (ss <= tt)).astype(np.float32)
    cb[:, ob["maskB"]:][:, :128] = (same & (ss >= tt)).astype(np.float32)
    cb[:, ob["cm"]:][:, :4] = ((np.arange(128)[:, None] // CH) == np.arange(4)[None, :]).astype(np.float32)
    for l in range(L):
        for g in range(4):
            cb[:, ob["sgw"] + (l * 4 + g) * 128:][:, :128] = np.asarray(inp["sg_w"][l][g], np.float32).T
            cb[:, ob["poolw"] + (l * 4 + g) * 128:][:, :128] = np.asarray(inp["pool_w"][l][g], np.float32)
    cb[:, ob["ident"]:][:, :128] = np.eye(128, dtype=np.float32)
    cb[:, ob["scanm"]:][:, :NT] = ((np.arange(NT) % CH) != 0).astype(np.float32)[None, :]
    return cf, cb


def host_x(inp, core):
    segs = core_segments(core)
    toks = []
    for kind, idx, off in segs:
        if kind == "s":
            toks.append(np.asarray(inp["x_sample"][idx][off:off + SEG], np.float32))
        else:
            toks.append(np.asarray(inp["x_prompt"][idx], np.float32))
    x = np.concatenate(toks, 0)
    return np.ascontiguousarray(x.T.reshape(8, 128, NT).transpose(1, 0, 2))


def build_program(nload, opts=None):
    opts = opts or {}
    NU = opts.get("NU", 40)
    nc = bass.Bass("TRN2", target_bir_lowering=False)
    x_in = nc.dram_tensor("xT", [128, 8, NT], F32, kind="ExternalInput").ap()
    cf_in = nc.dram_tensor("cf", [128, CFL.n], F32, kind="ExternalInput").ap()
    cb_in = nc.dram_tensor("cb", [128, CBL.n], F32, kind="ExternalInput").ap()
    w_hbm = nc.dram_tensor("wts", [nload, 128, SLOT_E], F32, kind="ExternalInput").ap()
    y_out = nc.dram_tensor("yT", [128, 8, NT], F32, kind="ExternalOutput").ap()
    s_out = nc.dram_tensor("snew", [L, NSEG, 2, 4, 128, 128], F32, kind="ExternalOutput").ap()
    dbg_out = None
    if opts.get("dbg"):
        dbg_out = nc.dram_tensor("dbg", [128, opts["dbg"]], F32, kind="ExternalOutput").ap()

    with ExitStack() as st:
        P = Prog(nc, st)
        sb = lambda name, shape, dt: st.enter_context(nc.sbuf_tensor(name, shape, dt))
        xT_t = [sb("xTs%d" % k, [128, NT], F32) for k in range(8)]
        xT = [Buf(t[:, :], [T("xT%d" % k)]) for k, t in enumerate(xT_t)]
        cf_t = sb("cf_sb", [128, CFL.n], F32)
        cb_t = sb("cb_sb", [128, CBL.n], BF16)
        big = sb("units", [128, NU * UW], BF16)
        U = UnitPool(P, big, NU)
        slot_all = sb("slots", [128, NSLOT, SLOT_E], BF16)
        slot_t = [slot_all[:, i, :] for i in range(NSLOT)]
        slot_T = [T("slot%d" % i) for i in range(NSLOT)]
        slot_sem = [P.new_sem("slot%d" % i) for i in range(NSLOT)]
        S_t = [sb("S%d" % i, [128, 128], F32) for i in range(8)]
        S_T = [T("S%d" % i) for i in range(8)]
        Sbf_t = [sb("Sbf%d" % i, [128, 2, 128], BF16) for i in range(8)]
        Sbf_T = [[T("Sbf%d_%d" % (i, j)) for j in range(2)] for i in range(8)]
        Asm_t = [sb("Asm%d" % i, [128, NCHK], F32) for i in range(4)]
        Asm_T = [T("Asm%d" % i) for i in range(4)]
        mod_t = sb("mod", [128, 48, NSEG], F32)
        mod_T = T("mod")
        gs_t = sb("gs", [128, 2, 8, NSEG], F32)
        gs_T = T("gs")
        lbv_t = sb("lbv", [128, 3, 8], F32)
        lbv_T = T("lbv")
        sc_t = sb("sc", [128, 8, NSEG], BF16)
        sc_T = T("sc")
        ones_d = sb("ones_d", [128, 128], BF16)
        ones_v = sb("ones_v", [128, 128], BF16)
        ones_T = T("ones")
        small_t = sb("small", [128, 64], F32)
        small_T = T("small")
        om_t = sb("omega", [128, 2], F32)
        om_T = T("omega")
        ktm_t = [sb("ktm%d" % i, [128, 4, 128], BF16) for i in range(4)]
        ktm_T = [T("ktm%d" % i) for i in range(4)]
        ktt_t = [sb("ktt%d" % i, [128, 128], BF16) for i in range(4)]
        ktt_T = [T("ktt%d" % i) for i in range(4)]
        scm_t = [sb("scm%d" % i, [128, 128], BF16) for i in range(4)]
        scm_T = [T("scm%d" % i) for i in range(4)]
        bank_t = [st.enter_context(nc.psum_tensor("bank%d" % i, [128, 512], F32)) for i in range(8)]
        bank_T = [T("bank%d" % i) for i in range(8)]
        bank_rr = [0]

        def bank():
            i = bank_rr[0] % 8
            bank_rr[0] += 1
            return Buf(bank_t[i][:, :], [bank_T[i]])

        dbg_off = [0]
        dbg_map = {}
        dsem = P.new_sem("dsem") if dbg_out is not None else None

        def dump(name, buf, width):
            if dbg_out is None or name not in opts.get("dump", ()):
                return
            o = dbg_off[0]
            dbg_off[0] += width
            dbg_map[name] = (o, width, str(buf.ap.dtype))
            if buf.ap.dtype != F32:
                tmpu = U.alloc(2)
                tb_ = sub(U.f32(tmpu), U.f32(tmpu).ap[:, 0:width])
                P.op("dve", lambda e, o_=tb_.ap, i_=buf.ap: e.tensor_copy(out=o_, in_=i_), reads=[buf], writes=[tb_])
                P.dma("sp", dbg_out[:, o:o + width], tb_.ap, dsem, reads=[tb_])
                U.release(tmpu)
            else:
                P.dma("sp", dbg_out[:, o:o + width], buf.ap, dsem, reads=[buf])
        DEBUG["dbg_map"] = dbg_map

        cfc = lambda name, off=0, w=1: cf_t[:, CFL.off[name] + off:CFL.off[name] + off + w]
        cbc = lambda name, off=0, w=1: cb_t[:, CBL.off[name] + off:CBL.off[name] + off + w]
        NOB = lambda ap: Buf(ap, [])

        def ACT(o, i, func, scale=1.0, bias=0.0, rd=(), accum=None):
            kw = {}
            wr = [o]
            if accum is not None:
                kw["accum_out"] = accum.ap
                wr.append(accum)
            P.op("act", lambda e, oa=o.ap, ia=i.ap: e.activation(out=oa, in_=ia, func=func, bias=bias, scale=scale, **kw),
                 reads=[i] + list(rd), writes=wr)

        def TT(o, a, b, op, eng="dve", rd=()):
            P.op(eng, lambda e, oa=o.ap, aa=a.ap, ba=b.ap: e.tensor_tensor(out=oa, in0=aa, in1=ba, op=op),
                 reads=[a, b] + list(rd), writes=[o])

        def TS(o, a, s1, s2, op0, op1=None, eng="dve", rd=()):
            if op1 is None:
                P.op(eng, lambda e, oa=o.ap, aa=a.ap: e.tensor_scalar(out=oa, in0=aa, scalar1=s1, scalar2=None, op0=op0),
                     reads=[a] + list(rd), writes=[o])
            else:
                P.op(eng, lambda e, oa=o.ap, aa=a.ap: e.tensor_scalar(out=oa, in0=aa, scalar1=s1, scalar2=s2, op0=op0, op1=op1),
                     reads=[a] + list(rd), writes=[o])

        def STT(o, a, s, b, op0, op1, eng="dve", rd=()):
            P.op(eng, lambda e, oa=o.ap, aa=a.ap, ba=b.ap: e.scalar_tensor_tensor(out=oa, in0=aa, scalar=s, in1=ba, op0=op0, op1=op1),
                 reads=[a, b] + list(rd), writes=[o])

        def MM(o, lhsT, rhs, start, stop, mark=False, skip=False):
            kw = {"skip_group_check": True} if skip else {}
            P.op("pe", lambda e, oa=o.ap, la=lhsT.ap, ra=rhs.ap: e.matmul(oa, la, ra, start=start, stop=stop, **kw),
                 reads=[lhsT, rhs], writes=[o], mark=mark)

        wstate = {"next_load": 0, "next_use": 0}

        def w_issue(slot):
            k = wstate["next_load"]
            if k >= nload:
                return
            wstate["next_load"] = k + 1
            P.dma("pool", slot_t[slot], w_hbm[k], slot_sem[slot], writes=[slot_T[slot]])

        def w_acquire():
            k = wstate["next_use"]
            wstate["next_use"] = k + 1
            assert k < wstate["next_load"], "weight slot not loaded"
            s = k % NSLOT
            return s

        def w_release(s):
            w_issue(s)

        def wk8(s):
            return Buf(slot_t[s].rearrange("p (k c) -> p k c", k=8), [slot_T[s]])

        def wk4(s):
            return Buf(slot_t[s].rearrange("p (k c) -> p k c", k=4), [slot_T[s]])

        csem = P.new_sem("const")
        for k in range(8):
            P.dma("sp", xT_t[k][:, :], x_in[:, k, :], csem)
        P.dma("sp", cf_t[:, :], cf_in[:, :], csem)
        cbsem = P.new_sem("cbsem")
        P.dma("pool", cb_t[:, :], cb_in[:, :], cbsem)
        for s in range(NSLOT):
            w_issue(s)
        for e in ("pe", "act", "dve", "pool"):
            P.wait_all(e, [csem, cbsem])
        ysems = [P.new_sem("ysem%d" % i) for i in range(2)]
        ssems = [P.new_sem("ssem%d" % i) for i in range(4)]

        P.op("dve", lambda e: e.memset(ones_d[:, :], 1.0 / D), writes=[ones_T])
        P.op("dve", lambda e: e.memset(ones_v[:, :], 1.0 / 128), writes=[ones_T])
        ONES_D = Buf(ones_d[:, :], [ones_T])
        ONES_V = Buf(ones_v[:, :], [ones_T])

        LBV = lambda i, c: Buf(lbv_t[:, i, c:c + 1], [lbv_T])
        lb_all = Buf(lbv_t[:, :, :], [lbv_T])
        TT(Buf(lbv_t[:, 0, :], [lbv_T]), NOB(cfc("lb", 8, 8)), NOB(cfc("lb", 0, 8)), ALU.subtract)
        ACT(Buf(lbv_t[:, 0, :], [lbv_T]), Buf(lbv_t[:, 0, :], [lbv_T]), AF.Sigmoid)
        TS(Buf(lbv_t[:, 1, :], [lbv_T]), Buf(lbv_t[:, 0, :], [lbv_T]), -1.0, 1.0, ALU.mult, ALU.add)
        TS(Buf(lbv_t[:, 2, :], [lbv_T]), Buf(lbv_t[:, 0, :], [lbv_T]), 1.0, -1.0, ALU.mult, ALU.add)

        SC = Buf(sc_t[:, :, :], [sc_T])
        ACT(Buf(sc_t[:, :, :].rearrange("p k s -> p (k s)"), [sc_T]), NOB(cfc("cT", 0, 40)), AF.Silu)

        if opts.get("pos", True):
            OM = Buf(om_t[:, :], [om_T])
            ACT(OM, NOB(cfc("jidx", 0, 2)), AF.Exp, scale=-float(np.log(10000.0)) / 256.0)
            TS(OM, OM, float(1.0 / (2 * np.pi)), 0.0, ALU.mult, ALU.add)
            eu = U.alloc(2)
            au = U.alloc(2)
            iu = U.alloc(2)
            E = sub(U.f32(eu), U.f32(eu).ap[:, 0:512])
            A = sub(U.f32(au), U.f32(au).ap[:, 0:512])
            KI = Buf(U.f32(iu).ap.bitcast(mybir.dt.int32)[:, 0:512], U.f32(iu).tiles)
            PF = sub(U.f32(iu), U.f32(iu).ap[:, 512:576])
            ki64 = sub(KI, KI.ap[:, 0:64])
            P.op("pool", lambda e, o=ki64.ap: e.iota(o, [[1, 64]], base=0, channel_multiplier=0), writes=[ki64])
            P.op("dve", lambda e, o=PF.ap, i=ki64.ap: e.tensor_copy(out=o, in_=i), reads=[ki64], writes=[PF])
            a3 = A.ap.rearrange("p (k i) -> p k i", k=8)
            e3 = E.ap.rearrange("p (k i) -> p k i", k=8)
            for kc in range(8):
                half = kc % 2
                c0 = 0.0 if (kc // 2) % 2 == 0 else 0.25
                TS(sub(A, a3[:, kc, :]), PF, om_t[:, half:half + 1], c0, ALU.mult, ALU.add, rd=[OM])
            P.op("dve", lambda e, o=KI.ap, i=A.ap: e.tensor_copy(out=o, in_=i), reads=[A], writes=[KI])
            P.op("dve", lambda e, o=E.ap, i=KI.ap: e.tensor_copy(out=o, in_=i), reads=[KI], writes=[E])
            TT(A, A, E, ALU.subtract)
            TS(E, A, 0.5, 1.0, ALU.is_gt, ALU.mult)
            TT(A, A, E, ALU.subtract)
            ACT(E, A, AF.Sin, scale=float(2 * np.pi))
            flag0 = cfc("flags", 0, 1)
            for kc in range(8):
                x3 = xT_t[kc][:, 0:1024].rearrange("p (r c) -> p r c", c=64)
                if kc < 4:
                    tab = e3[:, kc, 0:16].unsqueeze(2).to_broadcast([128, 16, 64])
                else:
                    tab = e3[:, kc, :].unsqueeze(1).to_broadcast([128, 16, 64])
                STT(sub(xT[kc], x3), sub(E, tab), flag0, sub(xT[kc], x3), ALU.mult, ALU.add)
            U.release(iu)
            U.release(eu)
            U.release(au)

        def square_x(k, squ):
            sq = U.bf(squ[k])
            ACT(sub(sq, sq.ap[:, 0:NT]), xT[k], AF.Square)

        def ssq_rstd(src_list, ones, n_k, rstd, squ=None):
            if squ is None:
                squ = [U.alloc(1) for _ in range(n_k)]
                for k in range(n_k):
                    sq = U.bf(squ[k])
                    ACT(sub(sq, sq.ap[:, 0:NT]), src_list[k], AF.Square)
            for (t0, tn) in TILES:
                b = bank()
                for k in range(n_k):
                    sq = U.bf(squ[k])
                    MM(sub(b, b.ap[:, 0:tn]), ones, sub(sq, sq.ap[:, t0:t0 + tn]), k == 0, k == n_k - 1, mark=(k == n_k - 1))
                rr = sub(rstd, rstd.ap[:, t0:t0 + tn])
                TS(rr, sub(b, b.ap[:, 0:tn]), 1.0, EPS, ALU.mult, ALU.add)
                ACT(rr, rr, AF.Ln)
                ACT(rr, rr, AF.Exp, scale=-0.5)
            for u in squ:
                U.release(u)

        def norm_mod(l, which, hT, squ=None):
            ru = U.alloc(2)
            rstd = U.f32(ru)
            ssq_rstd(xT, ONES_D, 8, rstd, squ)
            tus = [U.alloc(2), U.alloc(2)]
            shift_c = 0 if which == 0 else 24
            for kc in range(8):
                tmp = U.f32(tus[kc % 2])
                TT(sub(tmp, tmp.ap[:, 0:NT]), xT[kc], sub(rstd, rstd.ap[:, 0:NT]), ALU.mult)
                for s in range(NSEG):
                    h = hT[kc]
                    ACT(sub(h, h.ap[:, s * SEG:(s + 1) * SEG]), sub(tmp, tmp.ap[:, s * SEG:(s + 1) * SEG]), AF.Identity,
                        scale=gs_t[:, which, kc, s:s + 1], bias=mod_t[:, shift_c + kc, s:s + 1], rd=[T_gs, T_mod])
            U.release(ru)
            for tu in tus:
                U.release(tu)

        T_gs = gs_T
        T_mod = mod_T

        def fm_chunk(slot, hT, nk=8, wsel=None):
            banks = [bank() for _ in TILES]
            W = wk8(slot) if wsel is None else wsel
            for k in range(nk):
                for ti, (t0, tn) in enumerate(TILES):
                    h = hT[k]
                    MM(sub(banks[ti], banks[ti].ap[:, 0:tn]), sub(W, W.ap[:, k, :]), sub(h, h.ap[:, t0:t0 + tn]),
                       k == 0, k == nk - 1, mark=(k == nk - 1))
            return banks

        def fm_chunk_gen(slot, hT, out):
            banks = [bank() for _ in TILES]
            W = wk8(slot)
            for k in range(8):
                for ti, (t0, tn) in enumerate(TILES):
                    h = hT[k]
                    MM(sub(banks[ti], banks[ti].ap[:, 0:tn]), sub(W, W.ap[:, k, :]), sub(h, h.ap[:, t0:t0 + tn]),
                       k == 0, k == 7, mark=(k == 7))
                yield
            out.extend(banks)

        def tm_group(slots, hT, tb):
            b = bank()
            s0 = slots[0]
            assert list(slots) == [s0, s0 + 1, s0 + 2, s0 + 3]
            w4 = slot_all[:, s0:s0 + 4, :].rearrange("p s (k c) -> p s k c", k=8)
            tl = [slot_T[s] for s in slots]
            for k in range(8):
                h = hT[k]
                MM(b, sub(h, h.ap[:, tb * 128:(tb + 1) * 128]), Buf(w4[:, :, k, :], tl), k == 0, k == 7, mark=(k == 7))
            return b

        MODB = lambda c, s: mod_t[:, c, s:s + 1]

        def resid_add(banks, d, gate_c, squ=None):
            for s in range(NSEG):
                bi, off = divmod(s * SEG, 512)
                b = banks[bi]
                xs = sub(xT[d], xT_t[d][:, s * SEG:(s + 1) * SEG])
                STT(xs, sub(b, b.ap[:, off:off + SEG]), MODB(gate_c + d, s), xs, ALU.mult, ALU.add, rd=[mod_T])
            if squ is not None:
                square_x(d, squ)

        def gelu(out, z, tmp=None):
            ACT(out, z, AF.Gelu_apprx_tanh)

        def hgrn_group(l, heads, hT, VT, OHG, ohg_u):
            QH, OA, qus, oaus, chains = {}, {}, [], [], []
            for h in heads:
                qu = U.alloc(1)
                qus.append(qu)
                QH[h] = sub(U.bf(qu), U.bf(qu).ap[:, 0:NT])
                s = w_acquire()
                banks = fm_chunk(s, hT)
                w_release(s)
                for ti, (t0, tn) in enumerate(TILES):
                    ACT(sub(QH[h], QH[h].ap[:, t0:t0 + tn]), sub(banks[ti], banks[ti].ap[:, 0:tn]), AF.Silu)
            gens = {h: [hgrn_prep_gen(l, d, h, hT, VT, QH[h], 2 * hi + d) for d in range(2)] for hi, h in enumerate(heads)}
            res = {h: [None, None] for h in heads}

            def advance(h, n):
                for _ in range(n):
                    for gi in range(2):
                        if res[h][gi] is None:
                            try:
                                next(gens[h][gi])
                            except StopIteration as e:
                                res[h][gi] = e.value

            advance(heads[0], 9)
            for hi, h in enumerate(heads):
                if hi + 1 < len(heads):
                    advance(heads[hi + 1], 8)
                advance(h, 64)
                if hi + 1 < len(heads):
                    advance(heads[hi + 1], 1)
                chains.extend(res[h])
            for qu in qus:
                U.release(qu)
            for h in heads:
                oau = U.alloc(2)
                oaus.append(oau)
                OA[h] = sub(U.f32(oau), U.f32(oau).ap[:, 0:NT])
            for c in chains:
                c["OA"] = OA[c["h"]]
            for step in range(NBLK):
                hgrn_step(chains, step)
            for c in chains:
                for u in c["units"]:
                    U.release(u)
            for hi, h in enumerate(heads):
                gu = U.alloc(1)
                GH = sub(U.bf(gu), U.bf(gu).ap[:, 0:NT])
                s = w_acquire()
                banks = fm_chunk(s, hT)
                w_release(s)
                for ti, (t0, tn) in enumerate(TILES):
                    ACT(sub(GH, GH.ap[:, t0:t0 + tn]), sub(banks[ti], banks[ti].ap[:, 0:tn]), AF.Silu)
                ru = U.alloc(2)
                rstd = U.f32(ru)
                ssq_rstd([OA[h]], ONES_V, 1, rstd)
                TT(OA[h], OA[h], sub(rstd, rstd.ap[:, 0:NT]), ALU.mult)
                ou = U.alloc(1)
                ohg_u.append(ou)
                OHG[h] = sub(U.bf(ou), U.bf(ou).ap[:, 0:NT])
                STT(OHG[h], OA[h], cfc("hg_norm%d" % l, 0, 1), GH, ALU.mult, ALU.mult)
                U.release(ru)
                U.release(gu)
                U.release(oaus[hi])

        def hgrn_prep_gen(l, d, h, hT, VT, QH, slot_i):
            ci = d * 4 + h
            OA = None
            s = w_acquire()
            banks = []
            yield from fm_chunk_gen(s, hT, banks)
            w_release(s)
            su, lu, bu = U.alloc(2), U.alloc(2), U.alloc(2)
            S1 = sub(U.f32(su), U.f32(su).ap[:, 0:NT])
            L1 = sub(U.f32(lu), U.f32(lu).ap[:, 0:NT])
            B1 = sub(U.f32(bu), U.f32(bu).ap[:, 0:NT])
            qtu, ktu, keu = U.alloc(1), U.alloc(1), U.alloc(1)
            QT = sub(U.bf(qtu), U.bf(qtu).ap[:, 0:NT])
            KT = sub(U.bf(ktu), U.bf(ktu).ap[:, 0:NT])
            KE = sub(U.bf(keu), U.bf(keu).ap[:, 0:NT])
            for ti, (t0, tn) in enumerate(TILES):
                ACT(sub(S1, S1.ap[:, t0:t0 + tn]), sub(banks[ti], banks[ti].ap[:, 0:tn]), AF.Sigmoid, scale=-1.0)
            if l == 0:
                omlb, nomlb, rdl = 1.0, -1.0, []
            else:
                omlb, nomlb, rdl = lbv_t[:, 1, ci:ci + 1], lbv_t[:, 2, ci:ci + 1], [lbv_T]
            yield
            ACT(L1, S1, AF.Ln, scale=nomlb, bias=1.0, rd=rdl)
            P.op("dve", lambda e, o=B1.ap, m=cbc("scanm", 0, NT), x=L1.ap: e.tensor_tensor_scan(
                out=o, data0=m, data1=x, initial=0.0, op0=ALU.mult, op1=ALU.add), reads=[L1], writes=[B1])
            yield
            v3 = lambda b: b.ap.rearrange("p (n c) -> p n c", c=CH)
            if d == 0:
                BB, E1 = B1, L1
                endc = CH - 1
            else:
                TT(L1, L1, B1, ALU.subtract)
                TT(sub(L1, v3(L1)), sub(L1, v3(L1)), sub(B1, v3(B1)[:, :, CH - 1:CH].to_broadcast([128, NCHK, CH])), ALU.add)
                BB, E1 = L1, B1
                endc = 0
            yield
            ASM = Buf(Asm_t[slot_i][:, :], [Asm_T[slot_i]])
            ACT(ASM, sub(BB, v3(BB)[:, :, endc]), AF.Exp)
            ACT(E1, BB, AF.Exp, scale=-1.0)
            STT(KT, S1, omlb, E1, ALU.mult, ALU.mult, rd=rdl)
            yield
            TT(sub(KE, v3(KE)), sub(KT, v3(KT)), sub(ASM, Asm_t[slot_i][:, :].unsqueeze(2).to_broadcast([128, NCHK, CH])), ALU.mult)
            yield
            ACT(E1, BB, AF.Exp)
            STT(QT, QH, HG_SCALE, E1, ALU.mult, ALU.mult)
            U.release(su)
            U.release(lu)
            U.release(bu)
            return dict(l=l, d=d, h=h, ci=ci, QT=QT, KT=KT, KE=KE, VT=VT, OA=OA, ASM=ASM, asm_t=Asm_t[slot_i], si=slot_i,
                        units=[qtu, ktu, keu], sbi=0)

        def hgrn_step(chains, step):
            IDENT = NOB(cbc("ident", 0, 128))
            info = []
            for c in chains:
                l, d, h, ci, si = c["l"], c["d"], c["h"], c["ci"], c["si"]
                QT, KT, KE, VT = c["QT"], c["KT"], c["KE"], c["VT"]
                SS = Buf(S_t[ci][:, :], [S_T[ci]])
                tb = step if d == 0 else NBLK - 1 - step
                seg = tb // 2
                first = (tb % 2 == 0) if d == 0 else (tb % 2 == 1)
                t0 = tb * 128
                ceng = "act" if (si < 3 if KTM_ON_POOL else si % 2 == 0) else "pool"

                def cast(o, i_, ceng=ceng):
                    if ceng == "act":
                        ACT(o, i_, AF.Copy)
                    else:
                        P.op("pool", lambda e, oa=o.ap, ia=i_.ap: e.tensor_copy(out=oa, in_=ia), reads=[i_], writes=[o])

                if first:
                    cflag = cfc("flags", (1 if d == 0 else 3) * NSEG + seg, 1)
                    uflag = cfc("flags", (2 if d == 0 else 4) * NSEG + seg, 1)
                    if step == 0:
                        TS(SS, NOB(cfc("s0", (l * 8 + ci) * 128, 128)), uflag, 0.0, ALU.mult, ALU.add)
                    else:
                        TS(SS, SS, cflag, 0.0, ALU.mult, ALU.add)
                        STT(SS, NOB(cfc("s0", (l * 8 + ci) * 128, 128)), uflag, SS, ALU.mult, ALU.add)
                    SB = Buf(Sbf_t[ci][:, c["sbi"], :], [Sbf_T[ci][c["sbi"]]])
                    cast(SB, SS)
                bmisc = Buf(bank_t[2 * si][:, :], [bank_T[2 * si]])
                bU = Buf(bank_t[2 * si + 1][:, :], [bank_T[2 * si + 1]])
                bsc = sub(bmisc, bmisc.ap[:, 0:128])
                btr_bf = sub(bmisc, bmisc.ap[:, 128:256])
                bo = sub(bmisc, bmisc.ap[:, 256:384])
                MM(bsc, sub(KT, KT.ap[:, t0:t0 + 128]), sub(QT, QT.ap[:, t0:t0 + 128]), True, True)
                MM(btr_bf, sub(KE, KE.ap[:, t0:t0 + 128]), IDENT, True, True, mark=True)
                info.append(dict(c=c, SS=SS, tb=tb, seg=seg, first=first, t0=t0, cast=cast, bsc=bsc, btr_bf=btr_bf, bo=bo, bU=bU))
            for it in info:
                c = it["c"]
                si, d, h = c["si"], c["d"], c["h"]
                MASK = NOB(cbc("maskF" if d == 0 else "maskB", 0, 128))
                SCM = Buf(scm_t[si][:, :], [scm_T[si]])
                TT(SCM, it["bsc"], MASK, ALU.mult)
                KTM = Buf(ktm_t[si][:, :, :], [ktm_T[si]])
                if KTM_ON_POOL:
                    KTT = Buf(ktt_t[si][:, :], [ktt_T[si]])
                    ACT(KTT, it["btr_bf"], AF.Copy)
                    TT(KTM, sub(KTT, ktt_t[si][:, :].unsqueeze(1).to_broadcast([128, 4, 128])),
                       NOB(cbc("cm", 0, 4).unsqueeze(2).to_broadcast([128, 4, 128])), ALU.mult, eng="pool")
                elif KTM_ON_ACT:
                    for n in range(4):
                        ACT(sub(KTM, ktm_t[si][:, n, :]), it["btr_bf"], AF.Copy, scale=cfc("cmf", n, 1), rd=[SCM])
                else:
                    TT(KTM, sub(it["btr_bf"], it["btr_bf"].ap.unsqueeze(1).to_broadcast([128, 4, 128])),
                       NOB(cbc("cm", 0, 4).unsqueeze(2).to_broadcast([128, 4, 128])), ALU.mult)
                it["SCM"], it["KTM"] = SCM, KTM
            for it in info:
                c = it["c"]
                si, h, tb = c["si"], c["h"], it["tb"]
                VT = c["VT"]
                vblk = sub(VT, VT.ap[:, tb * 512 + h * 128: tb * 512 + (h + 1) * 128])
                it["vblk"] = vblk
                for n in range(4):
                    MM(sub(it["bU"], it["bU"].ap[:, n * 128:(n + 1) * 128]), sub(it["KTM"], ktm_t[si][:, n, :]), vblk, True, True, mark=(n == 3))
            for idx in range(4):
                for it in info:
                    c = it["c"]
                    si, d, ci, QT = c["si"], c["d"], c["ci"], c["QT"]
                    n = idx if d == 0 else 3 - idx
                    tb, t0, SS, bo, bU = it["tb"], it["t0"], it["SS"], it["bo"], it["bU"]
                    cidx = tb * 4 + n
                    SB = Buf(Sbf_t[ci][:, c["sbi"], :], [Sbf_T[ci][c["sbi"]]])
                    oc = sub(bo, bo.ap[:, n * CH:(n + 1) * CH])
                    MM(oc, it["vblk"], sub(it["SCM"], scm_t[si][:, n * CH:(n + 1) * CH]), True, False)
                    MM(oc, SB, sub(QT, QT.ap[:, t0 + n * CH:t0 + (n + 1) * CH]), False, True, mark=True)
                    STT(SS, SS, c["asm_t"][:, cidx:cidx + 1], sub(bU, bU.ap[:, n * 128:(n + 1) * 128]), ALU.mult, ALU.add, rd=[c["ASM"]])
                    c["sbi"] = 1 - c["sbi"]
                    SB2 = Buf(Sbf_t[ci][:, c["sbi"], :], [Sbf_T[ci][c["sbi"]]])
                    it["cast"](SB2, SS)
            for it in info:
                c = it["c"]
                l, d, h, ci, si, OA = c["l"], c["d"], c["h"], c["ci"], c["si"], c["OA"]
                tb, t0 = it["tb"], it["t0"]
                oa = sub(OA, OA.ap[:, t0:t0 + 128])
                first_writer = (tb < NBLK // 2) if d == 0 else (tb >= NBLK // 2)
                if first_writer:
                    ACT(oa, it["bo"], AF.Copy)
                else:
                    TT(oa, oa, it["bo"], ALU.add)
                if not it["first"]:
                    P.dma("sp", s_out[l, it["seg"], d, h], S_t[ci][:, :], ssems[si], reads=[it["SS"]])

        def sg_branch(l, hT, OSG):
            uu = [U.alloc(1) for _ in range(4)]
            UT = [sub(U.bf(u), U.bf(u).ap[:, 0:NT]) for u in uu]
            tu = U.alloc(2)
            TMP = sub(U.f32(tu), U.f32(tu).ap[:, 0:NT])
            for c in range(4):
                s = w_acquire()
                banks = fm_chunk(s, hT)
                w_release(s)
                for ti, (t0, tn) in enumerate(TILES):
                    gelu(sub(UT[c], UT[c].ap[:, t0:t0 + tn]), sub(banks[ti], banks[ti].ap[:, 0:tn]), sub(TMP, TMP.ap[:, t0:t0 + tn]))
            vnu = U.alloc(4)
            VN = U.bf(vnu)
            gvu = U.alloc(8)
            GV = U.f32(gvu)
            slots = [w_acquire() for _ in range(4)]
            SM = Buf(small_t[:, :], [small_T])
            for tb in range(NBLK):
                b = tm_group(slots, hT, tb)
                g2 = sub(GV, GV.ap[:, tb * 512:(tb + 1) * 512])
                g1 = sub(TMP, TMP.ap[:, (tb % 2) * 512:(tb % 2) * 512 + 512])
                gelu(g2, b)
                ssq = sub(SM, small_t[:, tb:tb + 1])
                ACT(g1, g2, AF.Square, accum=ssq)
            for s in slots:
                w_release(s)
            ssa = sub(SM, small_t[:, 0:NBLK])
            TS(ssa, ssa, 1.0 / 512, EPS, ALU.mult, ALU.add)
            ACT(ssa, ssa, AF.Sqrt)
            P.op("dve", lambda e, o=ssa.ap: e.reciprocal(out=o, in_=o), reads=[ssa], writes=[ssa])
            for tb in range(NBLK):
                g2 = sub(GV, GV.ap[:, tb * 512:(tb + 1) * 512])
                TS(sub(VN, VN.ap[:, tb * 512:(tb + 1) * 512]), g2, small_t[:, tb:tb + 1], 0.0, ALU.mult, ALU.add, rd=[SM])
            U.release(gvu)
            for tb in range(NBLK):
                t0 = tb * 128
                b = bank()
                for g in range(4):
                    MM(sub(b, b.ap[:, g * 128:(g + 1) * 128]), sub(VN, VN.ap[:, tb * 512 + g * 128:tb * 512 + (g + 1) * 128]),
                       NOB(cbc("sgw", (l * 4 + g) * 128, 128)), True, True, mark=(g == 3))
                for g in range(4):
                    t1 = sub(TMP, TMP.ap[:, g * 128:(g + 1) * 128])
                    STT(t1, sub(b, b.ap[:, g * 128:(g + 1) * 128]), cfc("sg_norm%d" % l, g, 1),
                        NOB(cfc("bbc", l * 512 + g * 128, 128)), ALU.mult, ALU.add)
                    TT(sub(OSG[g], OSG[g].ap[:, t0:t0 + 128]), t1, sub(UT[g], UT[g].ap[:, t0:t0 + 128]), ALU.mult)
            U.release(vnu)
            U.release(tu)
            for u in uu:
                U.release(u)

        def pool_branch(l, hT, OPL4):
            pu = U.alloc(4)
            PT = U.bf(pu)
            slots = [w_acquire() for _ in range(4)]
            for tb in range(NBLK):
                b = tm_group(slots, hT, tb)
                ACT(sub(PT, PT.ap[:, tb * 512:(tb + 1) * 512]), b, AF.Copy)
            for s in slots:
                w_release(s)
            plu = U.alloc(1)
            PLB = U.bf(plu)
            HS = Buf(small_t[:, :], [small_T])
            opl3 = OPL4.ap.rearrange("p (g u) -> p g u", g=4)
            for tb in range(NBLK):
                t0 = tb * 128
                tp = max(tb - 1, 0)
                tn_ = min(tb + 1, NBLK - 1)
                col = lambda t, g: sub(PT, PT.ap[:, t * 512 + g * 128:t * 512 + (g + 1) * 128])
                bm = bank()
                bh = bank()
                for g in range(4):
                    MM(sub(bm, bm.ap[:, g * 128:(g + 1) * 128]), col(tb, g), NOB(cbc("bmain", (tb * 4 + g) * 128, 128)), True, True)
                for g in range(4):
                    MM(sub(bh, bh.ap[:, g * 16:g * 16 + 8]), col(tp, g), NOB(cbc("bprev", (tb * 4 + g) * 8, 8)), True, True)
                    MM(sub(bh, bh.ap[:, g * 16 + 8:g * 16 + 16]), col(tn_, g), NOB(cbc("bnext", (tb * 4 + g) * 8, 8)), True, True, mark=(g == 3))
                o_ = (tb % 2) * 512
                PL = sub(PLB, PLB.ap[:, o_:o_ + 512])
                pl3 = PL.ap.rearrange("p (g t) -> p g t", g=4)
                bm3 = bm.ap.rearrange("p (g t) -> p g t", g=4)
                hs3 = small_t[:, 0:64].rearrange("p (g t) -> p g t", g=4)
                ACT(PL, bm, AF.Copy)
                ACT(sub(HS, small_t[:, 0:64]), sub(bh, bh.ap[:, 0:64]), AF.Copy)
                TT(sub(PL, pl3[:, :, 0:8]), sub(bm, bm3[:, :, 0:8]), sub(HS, hs3[:, :, 0:8]), ALU.add)
                TT(sub(PL, pl3[:, :, 120:128]), sub(bm, bm3[:, :, 120:128]), sub(HS, hs3[:, :, 8:16]), ALU.add)
                b2 = bank()
                for g in range(4):
                    MM(sub(b2, b2.ap[:, g * 128:(g + 1) * 128]), NOB(cbc("poolw", (l * 4 + g) * 128, 128)),
                       sub(PL, PL.ap[:, g * 128:(g + 1) * 128]), True, True, mark=(g == 3))
                TT(sub(OPL4, opl3[:, :, t0:t0 + 128]), sub(b2, b2.ap.rearrange("p (g t) -> p g t", g=4)),
                   NOB(cfc("pool_scale%d" % l, 0, 4).unsqueeze(2).to_broadcast([128, 4, 128])), ALU.mult)
            U.release(plu)
            U.release(pu)

        def merge(l, hT, BR, MG):
            au = [U.alloc(2), U.alloc(2)]
            ACCS = [sub(U.f32(u), U.f32(u).ap[:, 0:NT]) for u in au]
            gus = [U.alloc(2), U.alloc(2)]
            GTS = [sub(U.f32(u), U.f32(u).ap[:, 0:NT]) for u in gus]
            unit = 0
            gi = 0
            for j in range(4):
                for br in range(3):
                    sw, sg0, sg1 = w_acquire(), w_acquire(), w_acquire()
                    W = wk4(sw)
                    for i in range(2):
                        dch = 2 * j + i
                        ACC = ACCS[i]
                        G = wk8(sg0 if i == 0 else sg1)
                        for ti, (t0, tn) in enumerate(TILES):
                            bG = bank()
                            bGv = sub(bG, bG.ap[:, 0:tn])
                            for k in range(8):
                                MM(bGv, sub(G, G.ap[:, k, :]), sub(hT[k], hT[k].ap[:, t0:t0 + tn]), k == 0, k == 7, mark=(k == 7))
                            bP = bank()
                            pp = sub(bP, bP.ap[:, 0:tn])
                            for k in range(4):
                                o_ = BR[br][k]
                                MM(pp, sub(W, W.ap[:, k, i * 128:(i + 1) * 128]), sub(o_, o_.ap[:, t0:t0 + tn]), k == 0, k == 3, mark=(k == 3))
                            GT = GTS[gi % 2]
                            gi += 1
                            gt = sub(GT, GT.ap[:, t0:t0 + tn])
                            ACT(gt, bGv, AF.Sigmoid)
                            acc = sub(ACC, ACC.ap[:, t0:t0 + tn])
                            if br == 0:
                                TT(acc, gt, pp, ALU.mult)
                            elif br == 1:
                                TT(gt, gt, pp, ALU.mult)
                                TT(acc, acc, gt, ALU.add)
                            else:
                                TT(gt, gt, pp, ALU.mult)
                                TT(sub(MG[dch], MG[dch].ap[:, t0:t0 + tn]), acc, gt, ALU.add)
                    for s in (sw, sg0, sg1):
                        w_release(s)
                    ada_chunks(l, ada_late(unit))
                    unit += 1
            for u in au + gus:
                U.release(u)

        def ffn(l, hT):
            acu = [U.alloc(1) for _ in range(NFC)]
            ACTT = [sub(U.bf(u), U.bf(u).ap[:, 0:NT]) for u in acu]
            hbus = [U.alloc(2), U.alloc(2), U.alloc(2)]
            HBS = [U.f32(u) for u in hbus]
            cu = [U.alloc(2), U.alloc(2)]
            CV = [sub(U.f32(u), U.f32(u).ap[:, 0:NT]) for u in cu]
            fl3 = lambda k, a, b: cfc("flags", k * NSEG + a, b - a).unsqueeze(2)
            def ffn_head(c):
                j, ab = divmod(c, 2)
                col = j + ab * NFC
                HB = HBS[c % 3]
                hb3 = HB.ap[:, 0:NSEG * 258].rearrange("p (s c) -> p s c", c=258)
                s = w_acquire()
                banks = fm_chunk(s, hT)
                w_release(s)
                for ti, (t0, tn) in enumerate(TILES):
                    ns = tn // SEG
                    s0 = t0 // SEG
                    ACT(sub(HB, hb3[:, s0:s0 + ns, 1:257]), sub(banks[ti], banks[ti].ap[:, 0:tn].rearrange("p (s c) -> p s c", c=SEG)), AF.Copy)
                P.op("pool", lambda e, o=hb3[:, 0:1, 0:1]: e.memset(o, 0.0), writes=[HB])
                P.op("pool", lambda e, o=hb3[:, 4:5, 257:258]: e.memset(o, 0.0), writes=[HB])
                TT(sub(HB, hb3[:, 1:5, 0:1]), sub(HB, hb3[:, 0:4, 256:257]), NOB(fl3(5, 1, 5)), ALU.mult, eng="pool")
                TT(sub(HB, hb3[:, 0:4, 257:258]), sub(HB, hb3[:, 1:5, 1:2]), NOB(fl3(6, 0, 4)), ALU.mult, eng="pool")

            def ffn_tail(c):
                j, ab = divmod(c, 2)
                col = j + ab * NFC
                HB = HBS[c % 3]
                hb3 = HB.ap[:, 0:NSEG * 258].rearrange("p (s c) -> p s c", c=258)
                cv = CV[ab]
                cv3 = cv.ap.rearrange("p (s c) -> p s c", c=SEG)
                w0 = cfc("conv_w%d" % l, 0 * 44 + col, 1)
                w1 = cfc("conv_w%d" % l, 1 * 44 + col, 1)
                w2 = cfc("conv_w%d" % l, 2 * 44 + col, 1)
                bb = cfc("conv_b%d" % l, col, 1)
                ACT(sub(cv, cv3), sub(HB, hb3[:, :, 1:257]), AF.Identity, scale=w1, bias=bb)
                STT(sub(cv, cv3), sub(HB, hb3[:, :, 0:256]), w0, sub(cv, cv3), ALU.mult, ALU.add)
                STT(sub(cv, cv3), sub(HB, hb3[:, :, 2:258]), w2, sub(cv, cv3), ALU.mult, ALU.add, eng=FFN_C3_ENG)
                if ab == 1:
                    ACT(CV[0], CV[0], AF.Silu)
                    TT(ACTT[j], CV[0], CV[1], ALU.mult, eng=FFN_MUL_ENG)

            for c in range(2 * NFC):
                ffn_head(c)
                if c > 0:
                    ffn_tail(c - 1)
            ffn_tail(2 * NFC - 1)
            for u in hbus:
                U.release(u)
            for u in cu:
                U.release(u)
            squ_n = [U.alloc(1) for _ in range(8)]
            for d in range(8):
                banks = [bank() for _ in TILES]
                for g in range(3):
                    s = w_acquire()
                    W = wk8(s)
                    nk = 8 if g < 2 else NFC - 16
                    for k in range(nk):
                        kk = g * 8 + k
                        for ti, (t0, tn) in enumerate(TILES):
                            a = ACTT[kk]
                            MM(sub(banks[ti], banks[ti].ap[:, 0:tn]), sub(W, W.ap[:, k, :]), sub(a, a.ap[:, t0:t0 + tn]),
                               kk == 0, kk == NFC - 1, mark=(kk == NFC - 1 or k == nk - 1))
                    w_release(s)
                resid_add(banks, d, 40, squ_n)
            for u in acu:
                U.release(u)
            return squ_n

        def ada_chunks(l, js):
            for j in js:
                s = w_acquire()
                W = wk8(s)
                pm = bank()
                pr = sub(pm, pm.ap[:, 0:NSEG])
                for k in range(8):
                    MM(pr, sub(W, W.ap[:, k, :]), sub(SC, sc_t[:, k, :]), k == 0, k == 7, mark=(k == 7))
                w_release(s)
                mj = Buf(mod_t[:, j, :], [mod_T])
                TS(mj, pr, 1.0, cfc("b_ada%d" % l, j, 1), ALU.mult, ALU.add)
                if 8 <= j < 16:
                    TS(Buf(gs_t[:, 0, j - 8, :], [gs_T]), mj, 1.0, cfc("norm_mix%d" % l, j - 8, 1), ALU.add, ALU.mult)
                if 32 <= j < 40:
                    TS(Buf(gs_t[:, 1, j - 32, :], [gs_T]), mj, 1.0, cfc("norm_ffn%d" % l, j - 32, 1), ALU.add, ALU.mult)

        for l in range(L):
            if l == 0:
                squ_next = [U.alloc(1) for _ in range(8)]
                for k in range(8):
                    square_x(k, squ_next)
            ada_chunks(l, range(16))

            hu = [U.alloc(1) for _ in range(8)]
            hT = [sub(U.bf(u), U.bf(u).ap[:, 0:NT]) for u in hu]
            if l == 0:
                dump('x0', xT[0], NT)
                dump('x5', xT[5], NT)
                dump('mod', Buf(mod_t[:, :, :].rearrange('p c s -> p (c s)'), [mod_T]), 240)
            norm_mod(l, 0, hT, squ_next)
            if l == 0:
                dump('h0', hT[0], NT)

            vt_u = U.alloc(4)
            VT = U.bf(vt_u)
            slots = [w_acquire() for _ in range(4)]
            for tb in range(NBLK):
                b = tm_group(slots, hT, tb)
                ACT(sub(VT, VT.ap[:, tb * 512:(tb + 1) * 512]), b, AF.Copy)
            for s in slots:
                w_release(s)
            ohg_u = []
            OHG = [None] * 4
            for grp in HG_GROUPS:
                hgrn_group(l, grp, hT, VT, OHG, ohg_u)
            if l == 0:
                dump('ohg0', OHG[0], NT)
                dump('vt', sub(VT, VT.ap[:, 0:512]), 512)
            U.release(vt_u)
            ctx = dict(l=l, hT=hT)
            osg_u = [U.alloc(1) for _ in range(4)]
            OSG = [sub(U.bf(u), U.bf(u).ap[:, 0:NT]) for u in osg_u]
            sg_branch(l, hT, OSG)
            opl_u4 = U.alloc(4)
            opl_u = [opl_u4]
            OPL4 = U.bf(opl_u4)
            OPL = [sub(OPL4, OPL4.ap[:, g * UW:g * UW + NT]) for g in range(4)]
            pool_branch(l, hT, OPL4)
            if l == 0:
                dump('osg0', OSG[0], NT)
                dump('opl0', OPL[0], NT)
                dump('opl3', OPL[3], NT)
            mg_u = [U.alloc(1) for _ in range(8)]
            MG = [sub(U.bf(u), U.bf(u).ap[:, 0:NT]) for u in mg_u]
            merge(l, hT, (OHG, OSG, OPL), MG)
            if l == 0:
                dump('mg0', MG[0], NT)
            for u in ohg_u + osg_u + opl_u:
                U.release(u)
            squ2 = [U.alloc(1) for _ in range(8)]
            for d in range(8):
                s = w_acquire()
                banks = fm_chunk(s, MG)
                w_release(s)
                resid_add(banks, d, 16, squ2)
            for u in mg_u:
                U.release(u)
            if l == 0:
                dump('xmid0', xT[0], NT)
            norm_mod(l, 1, hT, squ2)
            squ_next = ffn(l, hT)
            if l == 0:
                dump('xl0', xT[0], NT)
            for u in hu:
                U.release(u)

        ru = U.alloc(2)
        rstd = U.f32(ru)
        ssq_rstd(xT, ONES_D, 8, rstd, squ_next)
        tu = [U.alloc(2) for _ in range(2)]
        for kc in range(8):
            tmp = U.f32(tu[kc % 2])
            tv = sub(tmp, tmp.ap[:, 0:NT])
            TT(tv, xT[kc], sub(rstd, rstd.ap[:, 0:NT]), ALU.mult)
            ACT(tv, tv, AF.Copy, scale=cfc("final_norm", kc, 1))
            P.dma("sp", y_out[:, kc, :], tv.ap, ysems[kc % 2], reads=[tv])
        P.wait_all("sp", ysems + ssems + ([dsem] if dsem is not None else []))
        for e in ("pe", "act", "dve"):
            pass
        if P.pending["pe"]:
            raise RuntimeError("pe pending")
        P.emit()
    return nc


_CACHE = {}


def kernel(**inputs):
    inp = {k: np.asarray(v) for k, v in inputs.items()}
    wts = host_weights(inp)
    nload = wts.shape[0]
    opts = dict(DEBUG.get("opts", {}))
    key = (nload, repr(sorted(opts.items())))
    if key not in _CACHE:
        _CACHE[key] = build_program(nload, opts)
    nc = _CACHE[key]
    in_maps = []
    for c in range(NCORE):
        cf, cb = host_consts(inp, c)
        in_maps.append({"xT": host_x(inp, c), "cf": cf, "cb": cb, "wts": wts})
    res = run_bass_kernel_spmd(nc, in_maps, core_ids=list(range(NCORE)))
    r = res.results
    DEBUG["results"] = r
    y_prompt = np.zeros((32, 256, D), np.float32)
    y_sample = np.zeros((2, 1024, D), np.float32)
    new_state = np.zeros((32, L, 2, 4, 128, 128), np.float32)
    for c in range(NCORE):
        yT = np.asarray(r[c]["yT"])
        y = yT.transpose(1, 0, 2).reshape(D, NT).T
        sn = np.asarray(r[c]["snew"])
        for s, (kind, idx, off) in enumerate(core_segments(c)):
            ys = y[s * SEG:(s + 1) * SEG]
            if kind == "s":
                y_sample[idx, off:off + SEG] = ys
            else:
                y_prompt[idx] = ys
                new_state[idx] = sn[:, s]
    return (y_prompt, y_sample, new_state)
```

```python
import numpy as np
import concourse.bass as bass
import concourse.mybir as mybir
from concourse.bass_utils import run_bass_kernel_spmd
from contextlib import ExitStack

F32 = mybir.dt.float32
BF16 = mybir.dt.bfloat16
AF = mybir.ActivationFunctionType
ALU = mybir.AluOpType

D = 1024
L = 2
NCORE = 8
NSEG = 5
SEG = 256
NT = NSEG * SEG
NBLK = NT // 128
CH = 32
NCHK = NT // CH
UW = 1296
TILES = [(0, 512), (512, 512), (1024, 256)]
EPS = 1e-6
DFF = 2816
NFC = DFF // 128
SLOT_E = 1024
NSLOT = 8
HG_SCALE = 128 ** -0.5
HG_GROUPS = ((0, 1), (2, 3))
FFN_C3_ENG = "dve"
KTM_ON_POOL = False
FFN_MUL_ENG = "dve"

DEBUG = {}


class Sem:
    def __init__(self, h, name):
        self.h = h
        self.name = name
        self.val = 0


class T:
    def __init__(self, name):
        self.name = name
        self.w = None
        self.r = {}


class Buf:
    def __init__(self, ap, tiles):
        self.ap = ap
        self.tiles = tiles


def _tiles(xs):
    out = []
    for x in xs:
        if x is None:
            continue
        if isinstance(x, T):
            out.append(x)
        elif isinstance(x, Buf):
            out.extend(x.tiles)
        elif isinstance(x, (list, tuple)):
            out.extend(_tiles(x))
        else:
            raise TypeError(type(x))
    return out


class Prog:
    ENG = ("pe", "act", "dve", "pool", "sp")

    def __init__(self, nc, stack):
        self.nc = nc
        self.q = {e: [] for e in self.ENG}
        self.sem = {}
        for e in ("pe", "act", "dve", "pool"):
            self.sem[e] = Sem(stack.enter_context(nc.semaphore("c_" + e)), e)
        self.known = {e: {} for e in self.ENG}
        self.pending = {e: False for e in self.ENG}
        self.stack = stack
        self.nsem = 4

    def new_sem(self, name):
        self.nsem += 1
        return Sem(self.stack.enter_context(self.nc.semaphore(name)), name)

    def _deps(self, eng, reads, writes):
        deps = {}

        def add(s, v):
            if deps.get(s, 0) < v:
                deps[s] = v

        for t in reads:
            if t.w is not None:
                add(*t.w)
        for t in writes:
            if t.w is not None:
                add(*t.w)
            for s, v in t.r.items():
                add(s, v)
        waits = []
        mysem = self.sem.get(eng)
        for s, v in deps.items():
            if s is mysem and eng == "pe":
                continue
            if self.known[eng].get(s, 0) >= v:
                continue
            self.known[eng][s] = v
            waits.append((s, v))
        return waits

    def op(self, eng, fn, reads=(), writes=(), mark=None):
        reads = _tiles(reads)
        writes = _tiles(writes)
        waits = self._deps(eng, reads, writes)
        s = self.sem[eng]
        if mark is None:
            mark = eng != "pe"
        n = s.val + 1
        if mark:
            s.val = n
            self.pending[eng] = False
        else:
            self.pending[eng] = True
        self.q[eng].append((waits, fn, (s, 1) if mark else None))
        for t in writes:
            t.w = (s, n)
            t.r = {}
        for t in reads:
            if t not in writes:
                t.r[s] = max(t.r.get(s, 0), n)

    def dma(self, eng, out, in_, sem, reads=(), writes=()):
        reads = _tiles(reads)
        writes = _tiles(writes)
        waits = self._deps(eng, reads, writes)
        sem.val += 16
        v = sem.val
        self.q[eng].append((waits, lambda e: e.dma_start(out=out, in_=in_), (sem, 16)))
        for t in writes:
            t.w = (sem, v)
            t.r = {}
        for t in reads:
            t.r[sem] = v

    def wait_all(self, eng, sems):
        for s in sems:
            if s.val > 0 and self.known[eng].get(s, 0) < s.val:
                self.known[eng][s] = s.val
                self.q[eng].append(([(s, s.val)], None, None))

    def emit(self):
        nc = self.nc
        for e in ("pe", "act", "dve", "pool"):
            assert not self.pending[e], "engine %s ends with unmarked op" % e
        with nc.Block() as block:

            def run(eng_name):
                def body(eng):
                    for waits, fn, inc in self.q[eng_name]:
                        for s, v in waits:
                            eng.wait_ge(s.h, v)
                        if fn is None:
                            continue
                        ins = fn(eng)
                        if inc is not None:
                            ins.then_inc(inc[0].h, inc[1])

                return body

            block.tensor(run("pe"))
            block.scalar(run("act"))
            block.vector(run("dve"))
            block.gpsimd(run("pool"))
            block.sync(run("sp"))


class UnitPool:
    def __init__(self, prog, big, nu):
        self.big = big
        self.nu = nu
        self.tiles = [T("u%d" % i) for i in range(nu)]
        self.free = [True] * nu
        self.peak = 0

    def alloc(self, n=1):
        for i in range(0, self.nu - n + 1):
            if all(self.free[i:i + n]):
                for j in range(i, i + n):
                    self.free[j] = False
                self.peak = max(self.peak, self.nu - sum(self.free))
                return (i, n)
        raise RuntimeError("unit pool exhausted (n=%d, free=%d)" % (n, sum(self.free)))

    def release(self, u):
        i, n = u
        for j in range(i, i + n):
            assert not self.free[j]
            self.free[j] = True

    def bf(self, u):
        i, n = u
        return Buf(self.big[:, i * UW:(i + n) * UW], self.tiles[i:i + n])

    def f32(self, u):
        i, n = u
        assert n % 2 == 0
        return Buf(self.big[:, i * UW:(i + n) * UW].bitcast(F32), self.tiles[i:i + n])


def sub(buf, ap):
    return Buf(ap, buf.tiles)


def _fm(v):
    v = np.asarray(v, np.float32)
    return np.ascontiguousarray(v.reshape(-1, 128).T)


def _w_slot_k8(w):
    return np.ascontiguousarray(w.reshape(8, 128, 128).transpose(1, 0, 2).reshape(128, 1024))


def _w_slot_k4(w):
    return np.ascontiguousarray(w.reshape(4, 128, 256).transpose(1, 0, 2).reshape(128, 1024))


class CF:
    def __init__(self):
        self.n = 0
        self.off = {}

    def add(self, name, width):
        self.off[name] = self.n
        self.n += width
        return self.off[name]


def build_layout():
    cf = CF()
    for l in range(L):
        cf.add("norm_mix%d" % l, 8)
        cf.add("norm_ffn%d" % l, 8)
        cf.add("b_ada%d" % l, 48)
        cf.add("hg_norm%d" % l, 1)
        cf.add("sg_norm%d" % l, 4)
        cf.add("pool_scale%d" % l, 4)
        cf.add("conv_w%d" % l, 3 * 44)
        cf.add("conv_b%d" % l, 44)
    cf.add("lb", L * 2 * 4)
    cf.add("final_norm", 8)
    cf.add("flags", 7 * NSEG)
    cf.add("cT", 8 * NSEG)
    cf.add("jidx", 2)
    cf.add("bbc", L * 4 * 128)
    cf.add("s0", L * 8 * 128)
    cb = CF()
    cb.add("bmain", 40 * 128)
    cb.add("bprev", 40 * 8)
    cb.add("bnext", 40 * 8)
    cb.add("maskF", 128)
    cb.add("maskB", 128)
    cb.add("cm", 4)
    cb.add("sgw", L * 4 * 128)
    cb.add("poolw", L * 4 * 128)
    cb.add("ident", 128)
    cb.add("scanm", NT)
    return cf, cb


CFL, CBL = build_layout()
POOL_W = (2, 4, 8, 16)


def ada_late(u):
    return range(16 + (u * 32) // 12, 16 + ((u + 1) * 32) // 12)


def weight_order(l):
    o = []
    for j in range(16):
        o.append(("ada", j))
    for j in range(4):
        o.append(("in", 12 + j))
    for grp in HG_GROUPS:
        for h in grp:
            o.append(("in", 0 + h))
        for h in grp:
            o += [("in", 4 + h), ("in", 8 + h)]
        for h in grp:
            o.append(("in", 16 + h))
    for j in range(4):
        o.append(("in", 20 + j))
    for j in range(4):
        o.append(("in", 24 + j))
    for j in range(4):
        o.append(("in", 28 + j))
    u = 0
    for j in range(4):
        for br in range(3):
            o.append(("br%d" % br, j))
            o.append(("in", 32 + br * 8 + 2 * j))
            o.append(("in", 32 + br * 8 + 2 * j + 1))
            for jj in ada_late(u):
                o.append(("ada", jj))
            u += 1
    for j in range(8):
        o.append(("out", j))
    for j in range(NFC):
        o.append(("up", j))
        o.append(("up", NFC + j))
    for d in range(8):
        for g in range(3):
            o.append(("down", d * 3 + g))
    return o


def host_weights(inp):
    slots = []
    for l in range(L):
        for kind, j in weight_order(l):
            if kind == "ada":
                s = _w_slot_k8(inp["w_ada"][l][:, j * 128:(j + 1) * 128])
            elif kind == "in":
                s = _w_slot_k8(inp["w_in"][l][:, j * 128:(j + 1) * 128])
            elif kind.startswith("br"):
                w = (inp["w_branch_hg"], inp["w_branch_sg"], inp["w_branch_pool"])[int(kind[2])][l]
                s = _w_slot_k4(w[:, j * 256:(j + 1) * 256])
            elif kind == "out":
                s = _w_slot_k8(inp["w_out"][l][:, j * 128:(j + 1) * 128])
            elif kind == "up":
                s = _w_slot_k8(inp["ffn_up"][l][:, j * 128:(j + 1) * 128])
            elif kind == "down":
                d, g = divmod(j, 3)
                wd = inp["ffn_down"][l][:, d * 128:(d + 1) * 128]
                blk = np.zeros((1024, 128), np.float32)
                k0 = g * 1024
                k1 = min(DFF, k0 + 1024)
                blk[:k1 - k0] = wd[k0:k1]
                s = _w_slot_k8(blk)
            slots.append(s)
    return np.ascontiguousarray(np.stack(slots, 0))


def band_matrices(seq_start, seq_end, w):
    hw = w // 2
    main = np.zeros((128, 128), np.float32)
    prev = np.zeros((128, 8), np.float32)
    nxt = np.zeros((128, 8), np.float32)
    for t in range(128):
        lo = t - hw
        hi = t + hw
        if seq_start:
            lo = max(lo, 0)
        if seq_end:
            hi = min(hi, 128)
        cnt = hi - lo
        for s in range(lo, hi):
            if s < 0:
                prev[128 + s, t] += 1.0 / cnt
            elif s >= 128:
                nxt[s - 128, t - 120] += 1.0 / cnt
            else:
                main[s, t] += 1.0 / cnt
        main[t, t] -= 1.0
    return main, prev, nxt


def core_segments(core):
    if core < 2:
        return [("s", core, i * SEG) for i in range(4)] + [("p", 30 + core, 0)]
    return [("p", (core - 2) * 5 + i, 0) for i in range(5)]


def host_consts(inp, core):
    segs = core_segments(core)
    cf = np.zeros((128, CFL.n), np.float32)
    o = CFL.off
    for l in range(L):
        cf[:, o["norm_mix%d" % l]:][:, :8] = _fm(inp["norm_mix"][l])
        cf[:, o["norm_ffn%d" % l]:][:, :8] = _fm(inp["norm_ffn"][l])
        cf[:, o["b_ada%d" % l]:][:, :48] = _fm(inp["b_ada"][l])
        cf[:, o["hg_norm%d" % l]:][:, :1] = _fm(inp["hg_norm"][l])
        cf[:, o["sg_norm%d" % l]:][:, :4] = _fm(inp["sg_norm"][l])
        cf[:, o["pool_scale%d" % l]:][:, :4] = _fm(inp["pool_scale"][l])
        for k in range(3):
            cf[:, o["conv_w%d" % l] + k * 44:][:, :44] = _fm(inp["ffn_conv_w"][l][k])
        cf[:, o["conv_b%d" % l]:][:, :44] = _fm(inp["ffn_conv_b"][l])
        for d in range(2):
            cf[:, o["lb"] + (l * 2 + d) * 4:][:, :4] = _fm(inp["lb_logits"][l][d])
        bb = np.asarray(inp["sg_b"][l], np.float32)
        cf[:, o["bbc"] + l * 512:][:, :512] = np.broadcast_to(bb.reshape(1, 512), (128, 512))
    cf[:, o["final_norm"]:][:, :8] = _fm(inp["final_norm"])
    fl = np.zeros((7, NSEG), np.float32)
    for s, (kind, idx, off) in enumerate(segs):
        if kind == "s":
            fl[0, s] = 1.0
            fl[1, s] = 1.0 if off > 0 else 0.0
            fl[2, s] = 1.0 if off == 0 else 0.0
            fl[3, s] = 1.0 if off < 768 else 0.0
            fl[4, s] = 1.0 if off == 768 else 0.0
            fl[5, s] = 1.0 if off > 0 else 0.0
            fl[6, s] = 1.0 if off < 768 else 0.0
    assert fl[0, 0] == fl[0, 1] == fl[0, 2] == fl[0, 3]
    cf[:, o["flags"]:][:, :7 * NSEG] = np.broadcast_to(fl.reshape(1, -1), (128, 7 * NSEG))
    cv = np.stack([inp["c"][idx] if kind == "s" else inp["c_ctx"] for kind, idx, off in segs], 0)
    cf[:, o["cT"]:][:, :40] = np.asarray(cv, np.float32).T.reshape(8, 128, NSEG).transpose(1, 0, 2).reshape(128, 40)
    cf[:, o["jidx"]] = np.arange(128)
    cf[:, o["jidx"] + 1] = np.arange(128) + 128
    if core < 2:
        st = np.asarray(inp["state_hgrn"][core], np.float32)
        cf[:, o["s0"]:][:, :L * 8 * 128] = st.reshape(L * 8, 128, 128).transpose(1, 0, 2).reshape(128, -1)

    cb = np.zeros((128, CBL.n), np.float32)
    ob = CBL.off
    for b in range(NBLK):
        s = b // 2
        kind, idx, off = segs[s]
        if kind == "s":
            pos = off + (b % 2) * 128
            st_, en_ = pos == 0, pos == 896
        else:
            st_, en_ = (b % 2 == 0), (b % 2 == 1)
        for g, w in enumerate(POOL_W):
            m, p, n = band_matrices(st_, en_, w)
            cb[:, ob["bmain"] + (b * 4 + g) * 128:][:, :128] = m
            cb[:, ob["bprev"] + (b * 4 + g) * 8:][:, :8] = p
            cb[:, ob["bnext"] + (b * 4 + g) * 8:][:, :8] = n
    ss = np.arange(128)[:, None]
    tt = np.arange(128)[None, :]
    same = (ss // CH) == (tt // CH)
    cb[:, ob["maskF"]:][:, :128] = (same & (ss <= tt)).astype(np.float32)
    cb[:, ob["maskB"]:][:, :128] = (same & (ss >= tt)).astype(np.float32)
    cb[:, ob["cm"]:][:, :4] = ((np.arange(128)[:, None] // CH) == np.arange(4)[None, :]).astype(np.float32)
    for l in range(L):
        for g in range(4):
            cb[:, ob["sgw"] + (l * 4 + g) * 128:][:, :128] = np.asarray(inp["sg_w"][l][g], np.float32).T
            cb[:, ob["poolw"] + (l * 4 + g) * 128:][:, :128] = np.asarray(inp["pool_w"][l][g], np.float32)
    cb[:, ob["ident"]:][:, :128] = np.eye(128, dtype=np.float32)
    cb[:, ob["scanm"]:][:, :NT] = ((np.arange(NT) % CH) != 0).astype(np.float32)[None, :]
    return cf, cb


def host_x(inp, core):
    segs = core_segments(core)
    toks = []
    for kind, idx, off in segs:
        if kind == "s":
            toks.append(np.asarray(inp["x_sample"][idx][off:off + SEG], np.float32))
        else:
            toks.append(np.asarray(inp["x_prompt"][idx], np.float32))
    x = np.concatenate(toks, 0)
    return np.ascontiguousarray(x.T.reshape(8, 128, NT).transpose(1, 0, 2))


def build_program(nload, opts=None):
    opts = opts or {}
    NU = opts.get("NU", 40)
    nc = bass.Bass("TRN2", target_bir_lowering=False)
    x_in = nc.dram_tensor("xT", [128, 8, NT], F32, kind="ExternalInput").ap()
    cf_in = nc.dram_tensor("cf", [128, CFL.n], F32, kind="ExternalInput").ap()
    cb_in = nc.dram_tensor("cb", [128, CBL.n], F32, kind="ExternalInput").ap()
    w_hbm = nc.dram_tensor("wts", [nload, 128, SLOT_E], F32, kind="ExternalInput").ap()
    y_out = nc.dram_tensor("yT", [128, 8, NT], F32, kind="ExternalOutput").ap()
    s_out = nc.dram_tensor("snew", [L, NSEG, 2, 4, 128, 128], F32, kind="ExternalOutput").ap()
    dbg_out = None
    if opts.get("dbg"):
        dbg_out = nc.dram_tensor("dbg", [128, opts["dbg"]], F32, kind="ExternalOutput").ap()

    with ExitStack() as st:
        P = Prog(nc, st)
        sb = lambda name, shape, dt: st.enter_context(nc.sbuf_tensor(name, shape, dt))
        xT_t = [sb("xTs%d" % k, [128, NT], F32) for k in range(8)]
        xT = [Buf(t[:, :], [T("xT%d" % k)]) for k, t in enumerate(xT_t)]
        cf_t = sb("cf_sb", [128, CFL.n], F32)
        cb_t = sb("cb_sb", [128, CBL.n], BF16)
        big = sb("units", [128, NU * UW], BF16)
        U = UnitPool(P, big, NU)
        slot_all = sb("slots", [128, NSLOT, SLOT_E], BF16)
        slot_t = [slot_all[:, i, :] for i in range(NSLOT)]
        slot_T = [T("slot%d" % i) for i in range(NSLOT)]
        slot_sem = [P.new_sem("slot%d" % i) for i in range(NSLOT)]
        S_t = [sb("S%d" % i, [128, 128], F32) for i in range(8)]
        S_T = [T("S%d" % i) for i in range(8)]
        Sbf_t = [sb("Sbf%d" % i, [128, 2, 128], BF16) for i in range(8)]
        Sbf_T = [[T("Sbf%d_%d" % (i, j)) for j in range(2)] for i in range(8)]
        Asm_t = [sb("Asm%d" % i, [128, NCHK], F32) for i in range(4)]
        Asm_T = [T("Asm%d" % i) for i in range(4)]
        mod_t = sb("mod", [128, 48, NSEG], F32)
        mod_T = T("mod")
        gs_t = sb("gs", [128, 2, 8, NSEG], F32)
        gs_T = T("gs")
        lbv_t = sb("lbv", [128, 3, 8], F32)
        lbv_T = T("lbv")
        sc_t = sb("sc", [128, 8, NSEG], BF16)
        sc_T = T("sc")
        ones_d = sb("ones_d", [128, 128], BF16)
        ones_v = sb("ones_v", [128, 128], BF16)
        ones_T = T("ones")
        small_t = sb("small", [128, 64], F32)
        small_T = T("small")
        om_t = sb("omega", [128, 2], F32)
        om_T = T("omega")
        ktm_t = [sb("ktm%d" % i, [128, 4, 128], BF16) for i in range(4)]
        ktm_T = [T("ktm%d" % i) for i in range(4)]
        ktt_t = [sb("ktt%d" % i, [128, 128], BF16) for i in range(4)]
        ktt_T = [T("ktt%d" % i) for i in range(4)]
        scm_t = [sb("scm%d" % i, [128, 128], BF16) for i in range(4)]
        scm_T = [T("scm%d" % i) for i in range(4)]
        bank_t = [st.enter_context(nc.psum_tensor("bank%d" % i, [128, 512], F32)) for i in range(8)]
        bank_T = [T("bank%d" % i) for i in range(8)]
        bank_rr = [0]

        def bank():
            i = bank_rr[0] % 8
            bank_rr[0] += 1
            return Buf(bank_t[i][:, :], [bank_T[i]])

        dbg_off = [0]
        dbg_map = {}
        dsem = P.new_sem("dsem") if dbg_out is not None else None

        def dump(name, buf, width):
            if dbg_out is None or name not in opts.get("dump", ()):
                return
            o = dbg_off[0]
            dbg_off[0] += width
            dbg_map[name] = (o, width, str(buf.ap.dtype))
            if buf.ap.dtype != F32:
                tmpu = U.alloc(2)
                tb_ = sub(U.f32(tmpu), U.f32(tmpu).ap[:, 0:width])
                P.op("dve", lambda e, o_=tb_.ap, i_=buf.ap: e.tensor_copy(out=o_, in_=i_), reads=[buf], writes=[tb_])
                P.dma("sp", dbg_out[:, o:o + width], tb_.ap, dsem, reads=[tb_])
                U.release(tmpu)
            else:
                P.dma("sp", dbg_out[:, o:o + width], buf.ap, dsem, reads=[buf])
        DEBUG["dbg_map"] = dbg_map

        cfc = lambda name, off=0, w=1: cf_t[:, CFL.off[name] + off:CFL.off[name] + off + w]
        cbc = lambda name, off=0, w=1: cb_t[:, CBL.off[name] + off:CBL.off[name] + off + w]
        NOB = lambda ap: Buf(ap, [])

        def ACT(o, i, func, scale=1.0, bias=0.0, rd=(), accum=None):
            kw = {}
            wr = [o]
            if accum is not None:
                kw["accum_out"] = accum.ap
                wr.append(accum)
            P.op("act", lambda e, oa=o.ap, ia=i.ap: e.activation(out=oa, in_=ia, func=func, bias=bias, scale=scale, **kw),
                 reads=[i] + list(rd), writes=wr)

        def TT(o, a, b, op, eng="dve", rd=()):
            P.op(eng, lambda e, oa=o.ap, aa=a.ap, ba=b.ap: e.tensor_tensor(out=oa, in0=aa, in1=ba, op=op),
                 reads=[a, b] + list(rd), writes=[o])

        def TS(o, a, s1, s2, op0, op1=None, eng="dve", rd=()):
            if op1 is None:
                P.op(eng, lambda e, oa=o.ap, aa=a.ap: e.tensor_scalar(out=oa, in0=aa, scalar1=s1, scalar2=None, op0=op0),
                     reads=[a] + list(rd), writes=[o])
            else:
                P.op(eng, lambda e, oa=o.ap, aa=a.ap: e.tensor_scalar(out=oa, in0=aa, scalar1=s1, scalar2=s2, op0=op0, op1=op1),
                     reads=[a] + list(rd), writes=[o])

        def STT(o, a, s, b, op0, op1, eng="dve", rd=()):
            P.op(eng, lambda e, oa=o.ap, aa=a.ap, ba=b.ap: e.scalar_tensor_tensor(out=oa, in0=aa, scalar=s, in1=ba, op0=op0, op1=op1),
                 reads=[a, b] + list(rd), writes=[o])

        def MM(o, lhsT, rhs, start, stop, mark=False, skip=False):
            kw = {"skip_group_check": True} if skip else {}
            P.op("pe", lambda e, oa=o.ap, la=lhsT.ap, ra=rhs.ap: e.matmul(oa, la, ra, start=start, stop=stop, **kw),
                 reads=[lhsT, rhs], writes=[o], mark=mark)

        wstate = {"next_load": 0, "next_use": 0}

        def w_issue(slot):
            k = wstate["next_load"]
            if k >= nload:
                return
            wstate["next_load"] = k + 1
            P.dma("pool", slot_t[slot], w_hbm[k], slot_sem[slot], writes=[slot_T[slot]])

        def w_acquire():
            k = wstate["next_use"]
            wstate["next_use"] = k + 1
            assert k < wstate["next_load"], "weight slot not loaded"
            s = k % NSLOT
            return s

        def w_release(s):
            w_issue(s)

        def wk8(s):
            return Buf(slot_t[s].rearrange("p (k c) -> p k c", k=8), [slot_T[s]])

        def wk4(s):
            return Buf(slot_t[s].rearrange("p (k c) -> p k c", k=4), [slot_T[s]])

        csem = P.new_sem("const")
        for k in range(8):
            P.dma("sp", xT_t[k][:, :], x_in[:, k, :], csem)
        P.dma("sp", cf_t[:, :], cf_in[:, :], csem)
        cbsem = P.new_sem("cbsem")
        for s in range(NSLOT):
            w_issue(s)
        P.dma("pool", cb_t[:, :], cb_in[:, :], cbsem)
        for e in ("pe", "act", "dve", "pool"):
            P.wait_all(e, [csem])
        ysems = [P.new_sem("ysem%d" % i) for i in range(2)]
        ssems = [P.new_sem("ssem%d" % i) for i in range(4)]

        P.op("dve", lambda e: e.memset(ones_d[:, :], 1.0 / D), writes=[ones_T])
        P.op("dve", lambda e: e.memset(ones_v[:, :], 1.0 / 128), writes=[ones_T])
        ONES_D = Buf(ones_d[:, :], [ones_T])
        ONES_V = Buf(ones_v[:, :], [ones_T])

        LBV = lambda i, c: Buf(lbv_t[:, i, c:c + 1], [lbv_T])
        lb_all = Buf(lbv_t[:, :, :], [lbv_T])
        TT(Buf(lbv_t[:, 0, :], [lbv_T]), NOB(cfc("lb", 8, 8)), NOB(cfc("lb", 0, 8)), ALU.subtract)
        ACT(Buf(lbv_t[:, 0, :], [lbv_T]), Buf(lbv_t[:, 0, :], [lbv_T]), AF.Sigmoid)
        TS(Buf(lbv_t[:, 1, :], [lbv_T]), Buf(lbv_t[:, 0, :], [lbv_T]), -1.0, 1.0, ALU.mult, ALU.add)
        TS(Buf(lbv_t[:, 2, :], [lbv_T]), Buf(lbv_t[:, 0, :], [lbv_T]), 1.0, -1.0, ALU.mult, ALU.add)

        SC = Buf(sc_t[:, :, :], [sc_T])
        ACT(Buf(sc_t[:, :, :].rearrange("p k s -> p (k s)"), [sc_T]), NOB(cfc("cT", 0, 40)), AF.Silu)

        if opts.get("pos", True):
            OM = Buf(om_t[:, :], [om_T])
            ACT(OM, NOB(cfc("jidx", 0, 2)), AF.Exp, scale=-float(np.log(10000.0)) / 256.0)
            TS(OM, OM, float(1.0 / (2 * np.pi)), 0.0, ALU.mult, ALU.add)
            eu = U.alloc(2)
            au = U.alloc(2)
            iu = U.alloc(2)
            E = sub(U.f32(eu), U.f32(eu).ap[:, 0:512])
            A = sub(U.f32(au), U.f32(au).ap[:, 0:512])
            KI = Buf(U.f32(iu).ap.bitcast(mybir.dt.int32)[:, 0:512], U.f32(iu).tiles)
            PF = sub(U.f32(iu), U.f32(iu).ap[:, 512:576])
            ki64 = sub(KI, KI.ap[:, 0:64])
            P.op("pool", lambda e, o=ki64.ap: e.iota(o, [[1, 64]], base=0, channel_multiplier=0), writes=[ki64])
            P.op("dve", lambda e, o=PF.ap, i=ki64.ap: e.tensor_copy(out=o, in_=i), reads=[ki64], writes=[PF])
            a3 = A.ap.rearrange("p (k i) -> p k i", k=8)
            e3 = E.ap.rearrange("p (k i) -> p k i", k=8)
            for kc in range(8):
                half = kc % 2
                c0 = 0.0 if (kc // 2) % 2 == 0 else 0.25
                TS(sub(A, a3[:, kc, :]), PF, om_t[:, half:half + 1], c0, ALU.mult, ALU.add, rd=[OM])
            P.op("dve", lambda e, o=KI.ap, i=A.ap: e.tensor_copy(out=o, in_=i), reads=[A], writes=[KI])
            P.op("dve", lambda e, o=E.ap, i=KI.ap: e.tensor_copy(out=o, in_=i), reads=[KI], writes=[E])
            TT(A, A, E, ALU.subtract)
            TS(E, A, 0.5, 1.0, ALU.is_gt, ALU.mult)
            TT(A, A, E, ALU.subtract)
            ACT(E, A, AF.Sin, scale=float(2 * np.pi))
            flag0 = cfc("flags", 0, 1)
            for kc in range(8):
                x3 = xT_t[kc][:, 0:1024].rearrange("p (r c) -> p r c", c=64)
                if kc < 4:
                    tab = e3[:, kc, 0:16].unsqueeze(2).to_broadcast([128, 16, 64])
                else:
                    tab = e3[:, kc, :].unsqueeze(1).to_broadcast([128, 16, 64])
                STT(sub(xT[kc], x3), sub(E, tab), flag0, sub(xT[kc], x3), ALU.mult, ALU.add)
            U.release(iu)
            U.release(eu)
            U.release(au)

        def square_x(k, squ):
            sq = U.bf(squ[k])
            ACT(sub(sq, sq.ap[:, 0:NT]), xT[k], AF.Square)

        def ssq_rstd(src_list, ones, n_k, rstd, squ=None):
            if squ is None:
                squ = [U.alloc(1) for _ in range(n_k)]
                for k in range(n_k):
                    sq = U.bf(squ[k])
                    ACT(sub(sq, sq.ap[:, 0:NT]), src_list[k], AF.Square)
            for (t0, tn) in TILES:
                b = bank()
                for k in range(n_k):
                    sq = U.bf(squ[k])
                    MM(sub(b, b.ap[:, 0:tn]), ones, sub(sq, sq.ap[:, t0:t0 + tn]), k == 0, k == n_k - 1, mark=(k == n_k - 1))
                rr = sub(rstd, rstd.ap[:, t0:t0 + tn])
                TS(rr, sub(b, b.ap[:, 0:tn]), 1.0, EPS, ALU.mult, ALU.add)
                ACT(rr, rr, AF.Ln)
                ACT(rr, rr, AF.Exp, scale=-0.5)
            for u in squ:
                U.release(u)

        def norm_mod(l, which, hT, squ=None):
            ru = U.alloc(2)
            rstd = U.f32(ru)
            ssq_rstd(xT, ONES_D, 8, rstd, squ)
            tus = [U.alloc(2), U.alloc(2)]
            shift_c = 0 if which == 0 else 24
            for kc in range(8):
                tmp = U.f32(tus[kc % 2])
                TT(sub(tmp, tmp.ap[:, 0:NT]), xT[kc], sub(rstd, rstd.ap[:, 0:NT]), ALU.mult)
                for s in range(NSEG):
                    h = hT[kc]
                    ACT(sub(h, h.ap[:, s * SEG:(s + 1) * SEG]), sub(tmp, tmp.ap[:, s * SEG:(s + 1) * SEG]), AF.Identity,
                        scale=gs_t[:, which, kc, s:s + 1], bias=mod_t[:, shift_c + kc, s:s + 1], rd=[T_gs, T_mod])
            U.release(ru)
            for tu in tus:
                U.release(tu)

        T_gs = gs_T
        T_mod = mod_T

        def fm_chunk(slot, hT, nk=8, wsel=None):
            banks = [bank() for _ in TILES]
            W = wk8(slot) if wsel is None else wsel
            for k in range(nk):
                for ti, (t0, tn) in enumerate(TILES):
                    h = hT[k]
                    MM(sub(banks[ti], banks[ti].ap[:, 0:tn]), sub(W, W.ap[:, k, :]), sub(h, h.ap[:, t0:t0 + tn]),
                       k == 0, k == nk - 1, mark=(k == nk - 1))
            return banks

        def fm_chunk_gen(slot, hT, out):
            banks = [bank() for _ in TILES]
            W = wk8(slot)
            for k in range(8):
                for ti, (t0, tn) in enumerate(TILES):
                    h = hT[k]
                    MM(sub(banks[ti], banks[ti].ap[:, 0:tn]), sub(W, W.ap[:, k, :]), sub(h, h.ap[:, t0:t0 + tn]),
                       k == 0, k == 7, mark=(k == 7))
                yield
            out.extend(banks)

        def tm_group(slots, hT, tb):
            b = bank()
            s0 = slots[0]
            assert list(slots) == [s0, s0 + 1, s0 + 2, s0 + 3]
            w4 = slot_all[:, s0:s0 + 4, :].rearrange("p s (k c) -> p s k c", k=8)
            tl = [slot_T[s] for s in slots]
            for k in range(8):
                h = hT[k]
                MM(b, sub(h, h.ap[:, tb * 128:(tb + 1) * 128]), Buf(w4[:, :, k, :], tl), k == 0, k == 7, mark=(k == 7))
            return b

        MODB = lambda c, s: mod_t[:, c, s:s + 1]

        def resid_add(banks, d, gate_c, squ=None):
            for s in range(NSEG):
                bi, off = divmod(s * SEG, 512)
                b = banks[bi]
                xs = sub(xT[d], xT_t[d][:, s * SEG:(s + 1) * SEG])
                STT(xs, sub(b, b.ap[:, off:off + SEG]), MODB(gate_c + d, s), xs, ALU.mult, ALU.add, rd=[mod_T])
            if squ is not None:
                square_x(d, squ)

        def gelu(out, z, tmp=None):
            ACT(out, z, AF.Gelu_apprx_tanh)

        def hgrn_group(l, heads, hT, VT, OHG, ohg_u):
            QH, OA, qus, oaus, chains = {}, {}, [], [], []
            for h in heads:
                qu = U.alloc(1)
                qus.append(qu)
                QH[h] = sub(U.bf(qu), U.bf(qu).ap[:, 0:NT])
                s = w_acquire()
                banks = fm_chunk(s, hT)
                w_release(s)
                for ti, (t0, tn) in enumerate(TILES):
                    ACT(sub(QH[h], QH[h].ap[:, t0:t0 + tn]), sub(banks[ti], banks[ti].ap[:, 0:tn]), AF.Silu)
            gens = {h: [hgrn_prep_gen(l, d, h, hT, VT, QH[h], 2 * hi + d) for d in range(2)] for hi, h in enumerate(heads)}
            res = {h: [None, None] for h in heads}

            def advance(h, n):
                for _ in range(n):
                    for gi in range(2):
                        if res[h][gi] is None:
                            try:
                                next(gens[h][gi])
                            except StopIteration as e:
                                res[h][gi] = e.value

            advance(heads[0], 9)
            for hi, h in enumerate(heads):
                if hi + 1 < len(heads):
                    advance(heads[hi + 1], 8)
                advance(h, 64)
                if hi + 1 < len(heads):
                    advance(heads[hi + 1], 1)
                chains.extend(res[h])
            for qu in qus:
                U.release(qu)
            for h in heads:
                oau = U.alloc(2)
                oaus.append(oau)
                OA[h] = sub(U.f32(oau), U.f32(oau).ap[:, 0:NT])
            for c in chains:
                c["OA"] = OA[c["h"]]
            for step in range(NBLK):
                hgrn_step(chains, step)
            for c in chains:
                for u in c["units"]:
                    U.release(u)
            for hi, h in enumerate(heads):
                gu = U.alloc(1)
                GH = sub(U.bf(gu), U.bf(gu).ap[:, 0:NT])
                s = w_acquire()
                banks = fm_chunk(s, hT)
                w_release(s)
                for ti, (t0, tn) in enumerate(TILES):
                    ACT(sub(GH, GH.ap[:, t0:t0 + tn]), sub(banks[ti], banks[ti].ap[:, 0:tn]), AF.Silu)
                ru = U.alloc(2)
                rstd = U.f32(ru)
                ssq_rstd([OA[h]], ONES_V, 1, rstd)
                TT(OA[h], OA[h], sub(rstd, rstd.ap[:, 0:NT]), ALU.mult)
                ou = U.alloc(1)
                ohg_u.append(ou)
                OHG[h] = sub(U.bf(ou), U.bf(ou).ap[:, 0:NT])
                STT(OHG[h], OA[h], cfc("hg_norm%d" % l, 0, 1), GH, ALU.mult, ALU.mult)
                U.release(ru)
                U.release(gu)
                U.release(oaus[hi])

        def hgrn_prep_gen(l, d, h, hT, VT, QH, slot_i):
            ci = d * 4 + h
            OA = None
            s = w_acquire()
            banks = []
            yield from fm_chunk_gen(s, hT, banks)
            w_release(s)
            su, lu, bu = U.alloc(2), U.alloc(2), U.alloc(2)
            S1 = sub(U.f32(su), U.f32(su).ap[:, 0:NT])
            L1 = sub(U.f32(lu), U.f32(lu).ap[:, 0:NT])
            B1 = sub(U.f32(bu), U.f32(bu).ap[:, 0:NT])
            qtu, ktu, keu = U.alloc(1), U.alloc(1), U.alloc(1)
            QT = sub(U.bf(qtu), U.bf(qtu).ap[:, 0:NT])
            KT = sub(U.bf(ktu), U.bf(ktu).ap[:, 0:NT])
            KE = sub(U.bf(keu), U.bf(keu).ap[:, 0:NT])
            for ti, (t0, tn) in enumerate(TILES):
                ACT(sub(S1, S1.ap[:, t0:t0 + tn]), sub(banks[ti], banks[ti].ap[:, 0:tn]), AF.Sigmoid, scale=-1.0)
            if l == 0:
                omlb, nomlb, rdl = 1.0, -1.0, []
            else:
                omlb, nomlb, rdl = lbv_t[:, 1, ci:ci + 1], lbv_t[:, 2, ci:ci + 1], [lbv_T]
            yield
            ACT(L1, S1, AF.Ln, scale=nomlb, bias=1.0, rd=rdl)
            P.op("dve", lambda e, o=B1.ap, m=cbc("scanm", 0, NT), x=L1.ap: e.tensor_tensor_scan(
                out=o, data0=m, data1=x, initial=0.0, op0=ALU.mult, op1=ALU.add), reads=[L1], writes=[B1])
            yield
            v3 = lambda b: b.ap.rearrange("p (n c) -> p n c", c=CH)
            if d == 0:
                BB, E1 = B1, L1
                endc = CH - 1
            else:
                TT(L1, L1, B1, ALU.subtract)
                TT(sub(L1, v3(L1)), sub(L1, v3(L1)), sub(B1, v3(B1)[:, :, CH - 1:CH].to_broadcast([128, NCHK, CH])), ALU.add)
                BB, E1 = L1, B1
                endc = 0
            yield
            ASM = Buf(Asm_t[slot_i][:, :], [Asm_T[slot_i]])
            ACT(ASM, sub(BB, v3(BB)[:, :, endc]), AF.Exp)
            ACT(E1, BB, AF.Exp, scale=-1.0)
            STT(KT, S1, omlb, E1, ALU.mult, ALU.mult, rd=rdl)
            yield
            TT(sub(KE, v3(KE)), sub(KT, v3(KT)), sub(ASM, Asm_t[slot_i][:, :].unsqueeze(2).to_broadcast([128, NCHK, CH])), ALU.mult)
            yield
            ACT(E1, BB, AF.Exp)
            STT(QT, QH, HG_SCALE, E1, ALU.mult, ALU.mult)
            U.release(su)
            U.release(lu)
            U.release(bu)
            return dict(l=l, d=d, h=h, ci=ci, QT=QT, KT=KT, KE=KE, VT=VT, OA=OA, ASM=ASM, asm_t=Asm_t[slot_i], si=slot_i,
                        units=[qtu, ktu, keu], sbi=0)

        def hgrn_step(chains, step):
            IDENT = NOB(cbc("ident", 0, 128))
            info = []
            for c in chains:
                l, d, h, ci, si = c["l"], c["d"], c["h"], c["ci"], c["si"]
                QT, KT, KE, VT = c["QT"], c["KT"], c["KE"], c["VT"]
                SS = Buf(S_t[ci][:, :], [S_T[ci]])
                tb = step if d == 0 else NBLK - 1 - step
                seg = tb // 2
                first = (tb % 2 == 0) if d == 0 else (tb % 2 == 1)
                t0 = tb * 128
                ceng = "act" if (si < 3 if KTM_ON_POOL else si % 2 == 0) else "pool"

                def cast(o, i_, ceng=ceng):
                    if ceng == "act":
                        ACT(o, i_, AF.Copy)
                    else:
                        P.op("pool", lambda e, oa=o.ap, ia=i_.ap: e.tensor_copy(out=oa, in_=ia), reads=[i_], writes=[o])

                if first:
                    cflag = cfc("flags", (1 if d == 0 else 3) * NSEG + seg, 1)
                    uflag = cfc("flags", (2 if d == 0 else 4) * NSEG + seg, 1)
                    if step == 0:
                        TS(SS, NOB(cfc("s0", (l * 8 + ci) * 128, 128)), uflag, 0.0, ALU.mult, ALU.add)
                    else:
                        TS(SS, SS, cflag, 0.0, ALU.mult, ALU.add)
                        STT(SS, NOB(cfc("s0", (l * 8 + ci) * 128, 128)), uflag, SS, ALU.mult, ALU.add)
                    SB = Buf(Sbf_t[ci][:, c["sbi"], :], [Sbf_T[ci][c["sbi"]]])
                    cast(SB, SS)
                bmisc = Buf(bank_t[2 * si][:, :], [bank_T[2 * si]])
                bU = Buf(bank_t[2 * si + 1][:, :], [bank_T[2 * si + 1]])
                bsc = sub(bmisc, bmisc.ap[:, 0:128])
                btr_bf = sub(bmisc, bmisc.ap[:, 128:192].bitcast(BF16))
                bo = sub(bmisc, bmisc.ap[:, 256:384])
                MM(bsc, sub(KT, KT.ap[:, t0:t0 + 128]), sub(QT, QT.ap[:, t0:t0 + 128]), True, True)
                P.op("pe", lambda e, o=btr_bf.ap, i=KE.ap[:, t0:t0 + 128], idn=IDENT.ap: e.transpose(o, i, idn),
                     reads=[KE], writes=[bmisc], mark=True)
                info.append(dict(c=c, SS=SS, tb=tb, seg=seg, first=first, t0=t0, cast=cast, bsc=bsc, btr_bf=btr_bf, bo=bo, bU=bU))
            for it in info:
                c = it["c"]
                si, d, h = c["si"], c["d"], c["h"]
                MASK = NOB(cbc("maskF" if d == 0 else "maskB", 0, 128))
                SCM = Buf(scm_t[si][:, :], [scm_T[si]])
                TT(SCM, it["bsc"], MASK, ALU.mult)
                KTM = Buf(ktm_t[si][:, :, :], [ktm_T[si]])
                if KTM_ON_POOL:
                    KTT = Buf(ktt_t[si][:, :], [ktt_T[si]])
                    ACT(KTT, it["btr_bf"], AF.Copy)
                    TT(KTM, sub(KTT, ktt_t[si][:, :].unsqueeze(1).to_broadcast([128, 4, 128])),
                       NOB(cbc("cm", 0, 4).unsqueeze(2).to_broadcast([128, 4, 128])), ALU.mult, eng="pool")
                else:
                    TT(KTM, sub(it["btr_bf"], it["btr_bf"].ap.unsqueeze(1).to_broadcast([128, 4, 128])),
                       NOB(cbc("cm", 0, 4).unsqueeze(2).to_broadcast([128, 4, 128])), ALU.mult)
                it["SCM"], it["KTM"] = SCM, KTM
            for it in info:
                c = it["c"]
                si, h, tb = c["si"], c["h"], it["tb"]
                VT = c["VT"]
                vblk = sub(VT, VT.ap[:, tb * 512 + h * 128: tb * 512 + (h + 1) * 128])
                it["vblk"] = vblk
                for n in range(4):
                    MM(sub(it["bU"], it["bU"].ap[:, n * 128:(n + 1) * 128]), sub(it["KTM"], ktm_t[si][:, n, :]), vblk, True, True, mark=(n == 3))
            for idx in range(4):
                for it in info:
                    c = it["c"]
                    si, d, ci, QT = c["si"], c["d"], c["ci"], c["QT"]
                    n = idx if d == 0 else 3 - idx
                    tb, t0, SS, bo, bU = it["tb"], it["t0"], it["SS"], it["bo"], it["bU"]
                    cidx = tb * 4 + n
                    SB = Buf(Sbf_t[ci][:, c["sbi"], :], [Sbf_T[ci][c["sbi"]]])
                    oc = sub(bo, bo.ap[:, n * CH:(n + 1) * CH])
                    MM(oc, it["vblk"], sub(it["SCM"], scm_t[si][:, n * CH:(n + 1) * CH]), True, False)
                    MM(oc, SB, sub(QT, QT.ap[:, t0 + n * CH:t0 + (n + 1) * CH]), False, True, mark=True)
                    STT(SS, SS, c["asm_t"][:, cidx:cidx + 1], sub(bU, bU.ap[:, n * 128:(n + 1) * 128]), ALU.mult, ALU.add, rd=[c["ASM"]])
                    c["sbi"] = 1 - c["sbi"]
                    SB2 = Buf(Sbf_t[ci][:, c["sbi"], :], [Sbf_T[ci][c["sbi"]]])
                    it["cast"](SB2, SS)
            for it in info:
                c = it["c"]
                l, d, h, ci, si, OA = c["l"], c["d"], c["h"], c["ci"], c["si"], c["OA"]
                tb, t0 = it["tb"], it["t0"]
                oa = sub(OA, OA.ap[:, t0:t0 + 128])
                first_writer = (tb < NBLK // 2) if d == 0 else (tb >= NBLK // 2)
                if first_writer:
                    ACT(oa, it["bo"], AF.Copy)
                else:
                    TT(oa, oa, it["bo"], ALU.add)
                if not it["first"]:
                    P.dma("sp", s_out[l, it["seg"], d, h], S_t[ci][:, :], ssems[si], reads=[it["SS"]])

        def sg_branch(l, hT, OSG):
            uu = [U.alloc(1) for _ in range(4)]
            UT = [sub(U.bf(u), U.bf(u).ap[:, 0:NT]) for u in uu]
            tu = U.alloc(2)
            TMP = sub(U.f32(tu), U.f32(tu).ap[:, 0:NT])
            for c in range(4):
                s = w_acquire()
                banks = fm_chunk(s, hT)
                w_release(s)
                for ti, (t0, tn) in enumerate(TILES):
                    gelu(sub(UT[c], UT[c].ap[:, t0:t0 + tn]), sub(banks[ti], banks[ti].ap[:, 0:tn]), sub(TMP, TMP.ap[:, t0:t0 + tn]))
            vnu = U.alloc(4)
            VN = U.bf(vnu)
            gvu = U.alloc(8)
            GV = U.f32(gvu)
            slots = [w_acquire() for _ in range(4)]
            SM = Buf(small_t[:, :], [small_T])
            for tb in range(NBLK):
                b = tm_group(slots, hT, tb)
                g2 = sub(GV, GV.ap[:, tb * 512:(tb + 1) * 512])
                g1 = sub(TMP, TMP.ap[:, (tb % 2) * 512:(tb % 2) * 512 + 512])
                gelu(g2, b)
                ssq = sub(SM, small_t[:, tb:tb + 1])
                ACT(g1, g2, AF.Square, accum=ssq)
            for s in slots:
                w_release(s)
            ssa = sub(SM, small_t[:, 0:NBLK])
            TS(ssa, ssa, 1.0 / 512, EPS, ALU.mult, ALU.add)
            ACT(ssa, ssa, AF.Sqrt)
            P.op("dve", lambda e, o=ssa.ap: e.reciprocal(out=o, in_=o), reads=[ssa], writes=[ssa])
            for tb in range(NBLK):
                g2 = sub(GV, GV.ap[:, tb * 512:(tb + 1) * 512])
                TS(sub(VN, VN.ap[:, tb * 512:(tb + 1) * 512]), g2, small_t[:, tb:tb + 1], 0.0, ALU.mult, ALU.add, rd=[SM])
            U.release(gvu)
            for tb in range(NBLK):
                t0 = tb * 128
                b = bank()
                for g in range(4):
                    MM(sub(b, b.ap[:, g * 128:(g + 1) * 128]), sub(VN, VN.ap[:, tb * 512 + g * 128:tb * 512 + (g + 1) * 128]),
                       NOB(cbc("sgw", (l * 4 + g) * 128, 128)), True, True, mark=(g == 3))
                for g in range(4):
                    t1 = sub(TMP, TMP.ap[:, g * 128:(g + 1) * 128])
                    STT(t1, sub(b, b.ap[:, g * 128:(g + 1) * 128]), cfc("sg_norm%d" % l, g, 1),
                        NOB(cfc("bbc", l * 512 + g * 128, 128)), ALU.mult, ALU.add)
                    TT(sub(OSG[g], OSG[g].ap[:, t0:t0 + 128]), t1, sub(UT[g], UT[g].ap[:, t0:t0 + 128]), ALU.mult)
            U.release(vnu)
            U.release(tu)
            for u in uu:
                U.release(u)

        def pool_branch(l, hT, OPL4):
            pu = U.alloc(4)
            PT = U.bf(pu)
            slots = [w_acquire() for _ in range(4)]
            for tb in range(NBLK):
                b = tm_group(slots, hT, tb)
                ACT(sub(PT, PT.ap[:, tb * 512:(tb + 1) * 512]), b, AF.Copy)
            for s in slots:
                w_release(s)
            plu = U.alloc(1)
            PLB = U.bf(plu)
            HS = Buf(small_t[:, :], [small_T])
            opl3 = OPL4.ap.rearrange("p (g u) -> p g u", g=4)
            def stage_a(tb):
                tp = max(tb - 1, 0)
                tn_ = min(tb + 1, NBLK - 1)
                col = lambda t, g: sub(PT, PT.ap[:, t * 512 + g * 128:t * 512 + (g + 1) * 128])
                bm = bank()
                bh = bank()
                for g in range(4):
                    MM(sub(bm, bm.ap[:, g * 128:(g + 1) * 128]), col(tb, g), NOB(cbc("bmain", (tb * 4 + g) * 128, 128)), True, True)
                for g in range(4):
                    MM(sub(bh, bh.ap[:, g * 16:g * 16 + 8]), col(tp, g), NOB(cbc("bprev", (tb * 4 + g) * 8, 8)), True, True)
                    MM(sub(bh, bh.ap[:, g * 16 + 8:g * 16 + 16]), col(tn_, g), NOB(cbc("bnext", (tb * 4 + g) * 8, 8)), True, True, mark=(g == 3))
                o_ = (tb % 2) * 512
                PL = sub(PLB, PLB.ap[:, o_:o_ + 512])
                pl3 = PL.ap.rearrange("p (g t) -> p g t", g=4)
                bm3 = bm.ap.rearrange("p (g t) -> p g t", g=4)
                hs3 = small_t[:, 0:64].rearrange("p (g t) -> p g t", g=4)
                ACT(PL, bm, AF.Copy)
                ACT(sub(HS, small_t[:, 0:64]), sub(bh, bh.ap[:, 0:64]), AF.Copy)
                TT(sub(PL, pl3[:, :, 0:8]), sub(bm, bm3[:, :, 0:8]), sub(HS, hs3[:, :, 0:8]), ALU.add)
                TT(sub(PL, pl3[:, :, 120:128]), sub(bm, bm3[:, :, 120:128]), sub(HS, hs3[:, :, 8:16]), ALU.add)
                return PL

            def stage_b(tb, PL):
                t0 = tb * 128
                b2 = bank()
                for g in range(4):
                    MM(sub(b2, b2.ap[:, g * 128:(g + 1) * 128]), NOB(cbc("poolw", (l * 4 + g) * 128, 128)),
                       sub(PL, PL.ap[:, g * 128:(g + 1) * 128]), True, True, mark=(g == 3))
                TT(sub(OPL4, opl3[:, :, t0:t0 + 128]), sub(b2, b2.ap.rearrange("p (g t) -> p g t", g=4)),
                   NOB(cfc("pool_scale%d" % l, 0, 4).unsqueeze(2).to_broadcast([128, 4, 128])), ALU.mult)

            pls = {}
            for tb in range(NBLK):
                pls[tb] = stage_a(tb)
                if tb > 0:
                    stage_b(tb - 1, pls.pop(tb - 1))
            stage_b(NBLK - 1, pls.pop(NBLK - 1))
            U.release(plu)
            U.release(pu)

        def merge(l, hT, BR, MG):
            au = [U.alloc(2), U.alloc(2)]
            ACCS = [sub(U.f32(u), U.f32(u).ap[:, 0:NT]) for u in au]
            gus = [U.alloc(2), U.alloc(2)]
            GTS = [sub(U.f32(u), U.f32(u).ap[:, 0:NT]) for u in gus]
            unit = 0
            gi = 0
            for j in range(4):
                for br in range(3):
                    sw, sg0, sg1 = w_acquire(), w_acquire(), w_acquire()
                    W = wk4(sw)
                    for i in range(2):
                        dch = 2 * j + i
                        ACC = ACCS[i]
                        G = wk8(sg0 if i == 0 else sg1)
                        for ti, (t0, tn) in enumerate(TILES):
                            bG = bank()
                            bGv = sub(bG, bG.ap[:, 0:tn])
                            for k in range(8):
                                MM(bGv, sub(G, G.ap[:, k, :]), sub(hT[k], hT[k].ap[:, t0:t0 + tn]), k == 0, k == 7, mark=(k == 7))
                            bP = bank()
                            pp = sub(bP, bP.ap[:, 0:tn])
                            for k in range(4):
                                o_ = BR[br][k]
                                MM(pp, sub(W, W.ap[:, k, i * 128:(i + 1) * 128]), sub(o_, o_.ap[:, t0:t0 + tn]), k == 0, k == 3, mark=(k == 3))
                            GT = GTS[gi % 2]
                            gi += 1
                            gt = sub(GT, GT.ap[:, t0:t0 + tn])
                            ACT(gt, bGv, AF.Sigmoid)
                            acc = sub(ACC, ACC.ap[:, t0:t0 + tn])
                            if br == 0:
                                TT(acc, gt, pp, ALU.mult)
                            elif br == 1:
                                TT(gt, gt, pp, ALU.mult)
                                TT(acc, acc, gt, ALU.add)
                            else:
                                TT(gt, gt, pp, ALU.mult)
                                TT(sub(MG[dch], MG[dch].ap[:, t0:t0 + tn]), acc, gt, ALU.add)
                    for s in (sw, sg0, sg1):
                        w_release(s)
                    ada_chunks(l, ada_late(unit))
                    unit += 1
            for u in au + gus:
                U.release(u)

        def ffn(l, hT):
            acu = [U.alloc(1) for _ in range(NFC)]
            ACTT = [sub(U.bf(u), U.bf(u).ap[:, 0:NT]) for u in acu]
            hbus = [U.alloc(2), U.alloc(2), U.alloc(2)]
            HBS = [U.f32(u) for u in hbus]
            cu = [U.alloc(2), U.alloc(2)]
            CV = [sub(U.f32(u), U.f32(u).ap[:, 0:NT]) for u in cu]
            fl3 = lambda k, a, b: cfc("flags", k * NSEG + a, b - a).unsqueeze(2)
            def ffn_head(c):
                j, ab = divmod(c, 2)
                col = j + ab * NFC
                HB = HBS[c % 3]
                hb3 = HB.ap[:, 0:NSEG * 258].rearrange("p (s c) -> p s c", c=258)
                s = w_acquire()
                banks = fm_chunk(s, hT)
                w_release(s)
                for ti, (t0, tn) in enumerate(TILES):
                    ns = tn // SEG
                    s0 = t0 // SEG
                    ACT(sub(HB, hb3[:, s0:s0 + ns, 1:257]), sub(banks[ti], banks[ti].ap[:, 0:tn].rearrange("p (s c) -> p s c", c=SEG)), AF.Copy)
                P.op("pool", lambda e, o=hb3[:, 0:1, 0:1]: e.memset(o, 0.0), writes=[HB])
                P.op("pool", lambda e, o=hb3[:, 4:5, 257:258]: e.memset(o, 0.0), writes=[HB])
                TT(sub(HB, hb3[:, 1:5, 0:1]), sub(HB, hb3[:, 0:4, 256:257]), NOB(fl3(5, 1, 5)), ALU.mult, eng="pool")
                TT(sub(HB, hb3[:, 0:4, 257:258]), sub(HB, hb3[:, 1:5, 1:2]), NOB(fl3(6, 0, 4)), ALU.mult, eng="pool")

            def ffn_tail(c):
                j, ab = divmod(c, 2)
                col = j + ab * NFC
                HB = HBS[c % 3]
                hb3 = HB.ap[:, 0:NSEG * 258].rearrange("p (s c) -> p s c", c=258)
                cv = CV[ab]
                cv3 = cv.ap.rearrange("p (s c) -> p s c", c=SEG)
                w0 = cfc("conv_w%d" % l, 0 * 44 + col, 1)
                w1 = cfc("conv_w%d" % l, 1 * 44 + col, 1)
                w2 = cfc("conv_w%d" % l, 2 * 44 + col, 1)
                bb = cfc("conv_b%d" % l, col, 1)
                ACT(sub(cv, cv3), sub(HB, hb3[:, :, 1:257]), AF.Identity, scale=w1, bias=bb)
                STT(sub(cv, cv3), sub(HB, hb3[:, :, 0:256]), w0, sub(cv, cv3), ALU.mult, ALU.add)
                STT(sub(cv, cv3), sub(HB, hb3[:, :, 2:258]), w2, sub(cv, cv3), ALU.mult, ALU.add, eng=FFN_C3_ENG)
                if ab == 1:
                    ACT(CV[0], CV[0], AF.Silu)
                    TT(ACTT[j], CV[0], CV[1], ALU.mult, eng=FFN_MUL_ENG)

            for c in range(2 * NFC):
                ffn_head(c)
                if c > 0:
                    ffn_tail(c - 1)
            ffn_tail(2 * NFC - 1)
            for u in hbus:
                U.release(u)
            for u in cu:
                U.release(u)
            squ_n = [U.alloc(1) for _ in range(8)]
            for d in range(8):
                banks = [bank() for _ in TILES]
                for g in range(3):
                    s = w_acquire()
                    W = wk8(s)
                    nk = 8 if g < 2 else NFC - 16
                    for k in range(nk):
                        kk = g * 8 + k
                        for ti, (t0, tn) in enumerate(TILES):
                            a = ACTT[kk]
                            MM(sub(banks[ti], banks[ti].ap[:, 0:tn]), sub(W, W.ap[:, k, :]), sub(a, a.ap[:, t0:t0 + tn]),
                               kk == 0, kk == NFC - 1, mark=(kk == NFC - 1 or k == nk - 1))
                    w_release(s)
                resid_add(banks, d, 40, squ_n)
            for u in acu:
                U.release(u)
            return squ_n

        def ada_chunks(l, js):
            for j in js:
                s = w_acquire()
                W = wk8(s)
                pm = bank()
                pr = sub(pm, pm.ap[:, 0:NSEG])
                for k in range(8):
                    MM(pr, sub(W, W.ap[:, k, :]), sub(SC, sc_t[:, k, :]), k == 0, k == 7, mark=(k == 7))
                w_release(s)
                mj = Buf(mod_t[:, j, :], [mod_T])
                TS(mj, pr, 1.0, cfc("b_ada%d" % l, j, 1), ALU.mult, ALU.add)
                if 8 <= j < 16:
                    TS(Buf(gs_t[:, 0, j - 8, :], [gs_T]), mj, 1.0, cfc("norm_mix%d" % l, j - 8, 1), ALU.add, ALU.mult)
                if 32 <= j < 40:
                    TS(Buf(gs_t[:, 1, j - 32, :], [gs_T]), mj, 1.0, cfc("norm_ffn%d" % l, j - 32, 1), ALU.add, ALU.mult)

        for l in range(L):
            if l == 0:
                squ_next = [U.alloc(1) for _ in range(8)]
                for k in range(8):
                    square_x(k, squ_next)
            ada_chunks(l, range(16))

            hu = [U.alloc(1) for _ in range(8)]
            hT = [sub(U.bf(u), U.bf(u).ap[:, 0:NT]) for u in hu]
            if l == 0:
                dump('x0', xT[0], NT)
                dump('x5', xT[5], NT)
                dump('mod', Buf(mod_t[:, :, :].rearrange('p c s -> p (c s)'), [mod_T]), 240)
            norm_mod(l, 0, hT, squ_next)
            if l == 0:
                dump('h0', hT[0], NT)

            if l == 0:
                for e in ("pe", "act", "dve", "pool"):
                    P.wait_all(e, [cbsem])
            vt_u = U.alloc(4)
            VT = U.bf(vt_u)
            slots = [w_acquire() for _ in range(4)]
            for tb in range(NBLK):
                b = tm_group(slots, hT, tb)
                ACT(sub(VT, VT.ap[:, tb * 512:(tb + 1) * 512]), b, AF.Copy)
            for s in slots:
                w_release(s)
            ohg_u = []
            OHG = [None] * 4
            for grp in HG_GROUPS:
                hgrn_group(l, grp, hT, VT, OHG, ohg_u)
            if l == 0:
                dump('ohg0', OHG[0], NT)
                dump('vt', sub(VT, VT.ap[:, 0:512]), 512)
            U.release(vt_u)
            ctx = dict(l=l, hT=hT)
            osg_u = [U.alloc(1) for _ in range(4)]
            OSG = [sub(U.bf(u), U.bf(u).ap[:, 0:NT]) for u in osg_u]
            sg_branch(l, hT, OSG)
            opl_u4 = U.alloc(4)
            opl_u = [opl_u4]
            OPL4 = U.bf(opl_u4)
            OPL = [sub(OPL4, OPL4.ap[:, g * UW:g * UW + NT]) for g in range(4)]
            pool_branch(l, hT, OPL4)
            if l == 0:
                dump('osg0', OSG[0], NT)
                dump('opl0', OPL[0], NT)
                dump('opl3', OPL[3], NT)
            mg_u = [U.alloc(1) for _ in range(8)]
            MG = [sub(U.bf(u), U.bf(u).ap[:, 0:NT]) for u in mg_u]
            merge(l, hT, (OHG, OSG, OPL), MG)
            if l == 0:
                dump('mg0', MG[0], NT)
            for u in ohg_u + osg_u + opl_u:
                U.release(u)
            squ2 = [U.alloc(1) for _ in range(8)]
            for d in range(8):
                s = w_acquire()
                banks = fm_chunk(s, MG)
                w_release(s)
                resid_add(banks, d, 16, squ2)
            for u in mg_u:
                U.release(u)
            if l == 0:
                dump('xmid0', xT[0], NT)
            norm_mod(l, 1, hT, squ2)
            squ_next = ffn(l, hT)
            if l == 0:
                dump('xl0', xT[0], NT)
            for u in hu:
                U.release(u)

        ru = U.alloc(2)
        rstd = U.f32(ru)
        ssq_rstd(xT, ONES_D, 8, rstd, squ_next)
        tu = [U.alloc(2) for _ in range(2)]
        for kc in range(8):
            tmp = U.f32(tu[kc % 2])
            tv = sub(tmp, tmp.ap[:, 0:NT])
            TT(tv, xT[kc], sub(rstd, rstd.ap[:, 0:NT]), ALU.mult)
            ACT(tv, tv, AF.Copy, scale=cfc("final_norm", kc, 1))
            P.dma("sp", y_out[:, kc, :], tv.ap, ysems[kc % 2], reads=[tv])
        P.wait_all("sp", ysems + ssems + ([dsem] if dsem is not None else []))
        for e in ("pe", "act", "dve"):
            pass
        if P.pending["pe"]:
            raise RuntimeError("pe pending")
        P.emit()
    return nc


_CACHE = {}


def kernel(**inputs):
    inp = {k: np.asarray(v) for k, v in inputs.items()}
    wts = host_weights(inp)
    nload = wts.shape[0]
    opts = dict(DEBUG.get("opts", {}))
    key = (nload, repr(sorted(opts.items())))
    if key not in _CACHE:
        _CACHE[key] = build_program(nload, opts)
    nc = _CACHE[key]
    in_maps = []
    for c in range(NCORE):
        cf, cb = host_consts(inp, c)
        in_maps.append({"xT": host_x(inp, c), "cf": cf, "cb": cb, "wts": wts})
    res = run_bass_kernel_spmd(nc, in_maps, core_ids=list(range(NCORE)))
    r = res.results
    DEBUG["results"] = r
    y_prompt = np.zeros((32, 256, D), np.float32)
    y_sample = np.zeros((2, 1024, D), np.float32)
    new_state = np.zeros((32, L, 2, 4, 128, 128), np.float32)
    for c in range(NCORE):
        yT = np.asarray(r[c]["yT"])
        y = yT.transpose(1, 0, 2).reshape(D, NT).T
        sn = np.asarray(r[c]["snew"])
        for s, (kind, idx, off) in enumerate(core_segments(c)):
            ys = y[s * SEG:(s + 1) * SEG]
            if kind == "s":
                y_sample[idx, off:off + SEG] = ys
            else:
                y_prompt[idx] = ys
                new_state[idx] = sn[:, s]
    return (y_prompt, y_sample, new_state)
```
